# Optimizing a Trainium2 kernel written in Bass

```python
import math
import jax, jax.numpy as jnp
from jax import lax
import numpy as np

D_MODEL = 1024
BATCH = 16
SEQ = 4096
DEPTH = 4

RET_HEADS = 4
RET_DK = 64
RET_DV = 64
RET_CHUNK = 128
RET_W = RET_HEADS * RET_DV
ROPE_BASE = 10000.0

S5_WIDTH = 256
S5_GROUP = 16
S5_GROUPS = S5_WIDTH // S5_GROUP
S5_STATE = 64
DT_MIN = 1e-3
DT_MAX = 1e-1

GLA_HEADS = 4
GLA_DK = 64
GLA_DV = 128
GLA_RANK = 16
GLA_TAU = 16.0
GLA_CHUNK = 64
GLA_KW = GLA_HEADS * GLA_DK
GLA_VW = GLA_HEADS * GLA_DV

N_BRANCH = 3
D_FF = 4 * D_MODEL
EPS = 1e-6

IN_SPLITS = (RET_HEADS * RET_DK, RET_HEADS * RET_DK, RET_W, RET_W, S5_WIDTH,
             GLA_KW, GLA_KW, GLA_VW, GLA_RANK, GLA_VW)
D_IN = 4 * RET_W + S5_WIDTH + 2 * GLA_KW + 2 * GLA_VW + GLA_RANK

kernel_name = "hybrid_retention_s5_gla_encoder"


def rmsnorm(x, g):
    xf = x.astype(jnp.float32)
    y = xf * lax.rsqrt(jnp.mean(xf * xf, axis=-1, keepdims=True) + EPS)
    return (y * g.astype(jnp.float32)).astype(x.dtype)


def head_norm(o, g, center, dtype):
    if center:
        o = o - jnp.mean(o, axis=-1, keepdims=True)
    o = o * lax.rsqrt(jnp.mean(o * o, axis=-1, keepdims=True) + EPS)
    Bn, L = o.shape[0], o.shape[1]
    return (o.reshape(Bn, L, -1) * g.astype(jnp.float32)).astype(dtype)


def rotary(x, pos):
    d = x.shape[-1]
    inv = 1.0 / (ROPE_BASE ** (jnp.arange(0, d, 2, dtype=jnp.float32) / d))
    ang = pos.astype(jnp.float32)[:, None] * inv[None, :]
    cos = jnp.cos(ang)[None, :, None, :]
    sin = jnp.sin(ang)[None, :, None, :]
    xf = x.astype(jnp.float32)
    x1, x2 = xf[..., : d // 2], xf[..., d // 2:]
    return jnp.concatenate([x1 * cos - x2 * sin, x1 * sin + x2 * cos], axis=-1)


def _retention_state_scan(kv, dec, reverse):
    def step(S, kv_n):
        return dec[None, :, None, None] * S + kv_n, S
    _, prev = lax.scan(step, jnp.zeros_like(kv[:, 0]), jnp.moveaxis(kv, 1, 0), reverse=reverse)
    return jnp.moveaxis(prev, 0, 1)


def retention_bidir(q, k, v):
    Bn, L, H, dk = q.shape
    dv = v.shape[-1]
    C = RET_CHUNK
    N = L // C
    f32 = jnp.float32
    q = q.astype(f32).reshape(Bn, N, C, H, dk) * (dk ** -0.5)
    k = k.astype(f32).reshape(Bn, N, C, H, dk)
    v = v.astype(f32).reshape(Bn, N, C, H, dv)
    log_g = jnp.log1p(-jnp.exp2(-5.0 - jnp.arange(H, dtype=f32)))
    idx = jnp.arange(C, dtype=f32)
    dmat = jnp.exp(log_g[:, None, None] * jnp.abs(idx[:, None] - idx[None, :]))
    s = jnp.einsum('bnihd,bnjhd->bnhij', q, k) * dmat
    o = jnp.einsum('bnhij,bnjhe->bnihe', s, v)
    w_f = jnp.exp(log_g[None, :] * (C - 1.0 - idx)[:, None])
    w_b = jnp.exp(log_g[None, :] * idx[:, None])
    kv_f = jnp.einsum('bnjhd,jh,bnjhe->bnhde', k, w_f, v)
    kv_b = jnp.einsum('bnjhd,jh,bnjhe->bnhde', k, w_b, v)
    dec = jnp.exp(log_g * C)
    S_f = _retention_state_scan(kv_f, dec, False)
    S_b = _retention_state_scan(kv_b, dec, True)
    q_f = jnp.exp(log_g[None, :] * (idx + 1.0)[:, None])
    q_b = jnp.exp(log_g[None, :] * (C - idx)[:, None])
    o = (o + jnp.einsum('bnihd,bnhde->bnihe', q * q_f[:, :, None], S_f)
           + jnp.einsum('bnihd,bnhde->bnihe', q * q_b[:, :, None], S_b))
    return o.reshape(Bn, L, H, dv)


def gla_chunked(q, k, v, log_a, strict):
    Bn, L, H, dk = q.shape
    dv = v.shape[-1]
    C = GLA_CHUNK
    N = L // C
    f32 = jnp.float32
    q = q.astype(f32).reshape(Bn, N, C, H, dk)
    k = k.astype(f32).reshape(Bn, N, C, H, dk)
    v = v.astype(f32).reshape(Bn, N, C, H, dv)
    b = jnp.cumsum(log_a.astype(f32).reshape(Bn, N, C, H, dk), axis=2)
    b_last = b[:, :, -1]
    q_in = q * jnp.exp(b)
    k_in = k * jnp.exp(-b)
    s = jnp.einsum('bnihd,bnjhd->bnhij', q_in, k_in)
    mask = jnp.tril(jnp.ones((C, C), dtype=bool), k=-1 if strict else 0)
    s = jnp.where(mask, s, 0.0)
    o = jnp.einsum('bnhij,bnjhe->bnihe', s, v)
    k_st = k * jnp.exp(b_last[:, :, None] - b)
    kv = jnp.einsum('bnjhd,bnjhe->bnhde', k_st, v)

    def step(S, inp):
        kv_n, dec_n = inp
        return dec_n[..., None] * S + kv_n, S

    _, S_prev = lax.scan(step, jnp.zeros_like(kv[:, 0]),
                         (jnp.moveaxis(kv, 1, 0), jnp.moveaxis(jnp.exp(b_last), 1, 0)))
    o = o + jnp.einsum('bnihd,bnhde->bnihe', q_in, jnp.moveaxis(S_prev, 0, 1))
    return o.reshape(Bn, L, H, dv)


def _lin_rec(e_i, e_j):
    a_i, b_i = e_i
    a_j, b_j = e_j
    return a_j * a_i, a_j * b_i + b_j


def s5_direction(u, lam_re, lam_im, log_dt, b_re, b_im, c_re, c_im, reverse):
    f32 = jnp.float32
    L = u.shape[1]
    lam = lax.complex(lam_re.astype(f32), lam_im.astype(f32))
    dt = jnp.exp(log_dt.astype(f32))[:, None]
    lam_bar = jnp.exp(lam * dt)
    Bm = lax.complex(b_re.astype(f32), b_im.astype(f32))
    B_bar = ((lam_bar - 1.0) / lam)[..., None] * Bm
    Bu = jnp.einsum('blgc,gpc->blgp', u.astype(jnp.complex64), B_bar)
    a = jnp.broadcast_to(lam_bar[None, None], (1, L) + lam_bar.shape)
    _, xs = lax.associative_scan(_lin_rec, (a, Bu), axis=1, reverse=reverse)
    Cm = lax.complex(c_re.astype(f32), c_im.astype(f32))
    return jnp.real(jnp.einsum('blgp,gcp->blgc', xs, Cm))


def setup_inputs(seed: int = 0) -> dict:
    key = jax.random.key(seed)
    ks = jax.random.split(key, 32)
    f32 = jnp.float32

    def nrm(k, shape, scale):
        return jax.random.normal(k, shape, f32) * scale

    G, P, Hc = S5_GROUPS, S5_STATE, S5_GROUP
    lam_im_base = (math.pi * jnp.arange(P, dtype=f32))[None, None, None, :]
    return {
        "x": nrm(ks[0], (BATCH, SEQ, D_MODEL), 1.0),
        "norm1_g": 1.0 + nrm(ks[1], (DEPTH, D_MODEL), 0.02),
        "w_in": nrm(ks[2], (DEPTH, D_MODEL, D_IN), D_MODEL ** -0.5),
        "ret_norm_g": 1.0 + nrm(ks[3], (DEPTH, RET_W), 0.02),
        "s5_lam_re": -0.5 + nrm(ks[4], (DEPTH, 2, G, P), 0.01),
        "s5_lam_im": lam_im_base + nrm(ks[5], (DEPTH, 2, G, P), 0.01),
        "s5_log_dt": jax.random.uniform(ks[6], (DEPTH, 2, G), f32, math.log(DT_MIN), math.log(DT_MAX)),
        "s5_b_re": nrm(ks[7], (DEPTH, 2, G, P, Hc), (2 * Hc) ** -0.5),
        "s5_b_im": nrm(ks[8], (DEPTH, 2, G, P, Hc), (2 * Hc) ** -0.5),
        "s5_c_re": nrm(ks[9], (DEPTH, 2, G, Hc, P), (2 * P) ** -0.5 * 4.0),
        "s5_c_im": nrm(ks[10], (DEPTH, 2, G, Hc, P), (2 * P) ** -0.5 * 4.0),
        "s5_d": nrm(ks[11], (DEPTH, S5_WIDTH), 1.0),
        "gla_w_gate": nrm(ks[12], (DEPTH, 2, GLA_RANK, GLA_KW), GLA_RANK ** -0.5),
        "gla_b_gate": nrm(ks[13], (DEPTH, 2, GLA_KW), 0.1),
        "gla_norm_g": 1.0 + nrm(ks[14], (DEPTH, GLA_VW), 0.02),
        "w_branch_a": nrm(ks[15], (DEPTH, RET_W, D_MODEL), RET_W ** -0.5),
        "w_branch_b": nrm(ks[16], (DEPTH, S5_WIDTH, 2 * D_MODEL), S5_WIDTH ** -0.5),
        "w_branch_c": nrm(ks[17], (DEPTH, GLA_VW, D_MODEL), GLA_VW ** -0.5),
        "w_merge_gate": nrm(ks[18], (DEPTH, D_MODEL, N_BRANCH * D_MODEL), D_MODEL ** -0.5),
        "b_merge_gate": nrm(ks[19], (DEPTH, N_BRANCH * D_MODEL), 0.1),
        "w_out": nrm(ks[20], (DEPTH, D_MODEL, D_MODEL), D_MODEL ** -0.5),
        "norm2_g": 1.0 + nrm(ks[21], (DEPTH, D_MODEL), 0.02),
        "w_ff1": nrm(ks[22], (DEPTH, D_MODEL, D_FF), D_MODEL ** -0.5),
        "w_ff2": nrm(ks[23], (DEPTH, D_FF, D_MODEL), D_FF ** -0.5),
        "final_norm_g": 1.0 + nrm(ks[24], (D_MODEL,), 0.02),
    }


def reference(x, norm1_g, w_in, ret_norm_g, s5_lam_re, s5_lam_im, s5_log_dt, s5_b_re, s5_b_im,
              s5_c_re, s5_c_im, s5_d, gla_w_gate, gla_b_gate, gla_norm_g, w_branch_a, w_branch_b,
              w_branch_c, w_merge_gate, b_merge_gate, w_out, norm2_g, w_ff1, w_ff2, final_norm_g):
    Bn, L, D = x.shape
    dt = x.dtype
    pos = jnp.arange(L)
    split_idx = [int(s) for s in np.cumsum(IN_SPLITS)[:-1]]
    for l in range(DEPTH):
        h = rmsnorm(x, norm1_g[l])
        z = h @ w_in[l]
        a_q, a_k, a_v, a_g, b_u, c_q, c_k, c_v, c_lr, c_r = jnp.split(z, split_idx, axis=-1)

        rq = rotary(a_q.reshape(Bn, L, RET_HEADS, RET_DK), pos)
        rk = rotary(a_k.reshape(Bn, L, RET_HEADS, RET_DK), pos)
        ro = retention_bidir(rq, rk, a_v.reshape(Bn, L, RET_HEADS, RET_DV))
        ro = head_norm(ro, ret_norm_g[l], True, dt) * jax.nn.silu(a_g)
        br_a = ro @ w_branch_a[l]

        u = b_u.astype(jnp.float32).reshape(Bn, L, S5_GROUPS, S5_GROUP)
        y = (s5_direction(u, s5_lam_re[l, 0], s5_lam_im[l, 0], s5_log_dt[l, 0], s5_b_re[l, 0],
                          s5_b_im[l, 0], s5_c_re[l, 0], s5_c_im[l, 0], False)
             + s5_direction(u, s5_lam_re[l, 1], s5_lam_im[l, 1], s5_log_dt[l, 1], s5_b_re[l, 1],
                            s5_b_im[l, 1], s5_c_re[l, 1], s5_c_im[l, 1], True)
             + s5_d[l].astype(jnp.float32).reshape(S5_GROUPS, S5_GROUP) * u)
        y = jax.nn.gelu(y.reshape(Bn, L, S5_WIDTH), approximate=False).astype(dt)
        glu = y @ w_branch_b[l]
        br_b = glu[..., :D] * jax.nn.sigmoid(glu[..., D:])

        gq = c_q.reshape(Bn, L, GLA_HEADS, GLA_DK) * (GLA_DK ** -0.5)
        gk = c_k.reshape(Bn, L, GLA_HEADS, GLA_DK)
        gv = c_v.reshape(Bn, L, GLA_HEADS, GLA_DV)
        lr = c_lr.astype(jnp.float32)
        la_f = (jax.nn.log_sigmoid(lr @ gla_w_gate[l, 0].astype(jnp.float32)
                                   + gla_b_gate[l, 0].astype(jnp.float32)) / GLA_TAU
                ).reshape(Bn, L, GLA_HEADS, GLA_DK)
        la_b = (jax.nn.log_sigmoid(lr @ gla_w_gate[l, 1].astype(jnp.float32)
                                   + gla_b_gate[l, 1].astype(jnp.float32)) / GLA_TAU
                ).reshape(Bn, L, GLA_HEADS, GLA_DK)
        go_f = gla_chunked(gq, gk, gv, la_f, False)
        go_b = jnp.flip(gla_chunked(jnp.flip(gq, 1), jnp.flip(gk, 1), jnp.flip(gv, 1),
                                    jnp.flip(la_b, 1), True), 1)
        go = head_norm(go_f + go_b, gla_norm_g[l], False, dt) * jax.nn.silu(c_r)
        br_c = go @ w_branch_c[l]

        gates = jax.nn.sigmoid(h @ w_merge_gate[l] + b_merge_gate[l])
        g_a, g_b, g_c = jnp.split(gates, N_BRANCH, axis=-1)
        x = x + (g_a * br_a + g_b * br_b + g_c * br_c) @ w_out[l]

        h2 = rmsnorm(x, norm2_g[l])
        x = x + jnp.square(jax.nn.relu(h2 @ w_ff1[l])) @ w_ff2[l]
    return rmsnorm(x, final_norm_g)
```

```python
import math
import numpy as np
import concourse.bass as bass
import concourse.mybir as mybir
from concourse.bass_utils import run_bass_kernel_spmd
from contextlib import ExitStack

F32 = mybir.dt.float32
BF16 = mybir.dt.bfloat16
I32 = mybir.dt.int32
AF = mybir.ActivationFunctionType
ALU = mybir.AluOpType
AX = mybir.AxisListType

D = 1024
DEPTH_FULL = 4
L_FULL = 4096
NCORES = 8
EPS = 1e-6
T1 = 8
NOCAST = False
import os
CST = int(os.environ.get('CST', '9'))
NFM = 1952
NTM = 2304
NIN2 = NFM + NTM


class Buf:
    __slots__ = ("w", "r")

    def __init__(self):
        self.w = None
        self.r = {}


class T:
    def __init__(self, t):
        self.t = t
        self.b = Buf()


class RR:
    def __init__(self, tiles):
        self.tiles = tiles
        self.i = 0

    def get(self):
        t = self.tiles[self.i % len(self.tiles)]
        self.i += 1
        return t


class Eng:
    def __init__(self, nc, obj, name, es):
        self.obj = obj
        self.name = name
        self.sem = es.enter_context(nc.semaphore("sem_" + name))
        self.count = 0
        self.seen = {}


class Prog:
    def __init__(self, nc, es, ndma=40):
        self.nc = nc
        self.pe = Eng(nc, nc.tensor, "pe", es)
        self.dve = Eng(nc, nc.vector, "dve", es)
        self.act = Eng(nc, nc.scalar, "act", es)
        self.pool = Eng(nc, nc.gpsimd, "pool", es)
        self.sp = Eng(nc, nc.sync, "sp", es)
        self.engs = [self.pe, self.dve, self.act, self.pool, self.sp]
        self.dma_sems = [es.enter_context(nc.semaphore("dsem%d" % i)) for i in range(ndma)]
        self.dma_val = [0] * ndma
        self.dma_next = 0
        self.uid = 0

    def name(self, p):
        self.uid += 1
        return "%s_%d" % (p, self.uid)

    def sb(self, es, shape, dt, nm="t"):
        return T(es.enter_context(self.nc.sbuf_tensor(self.name(nm), list(shape), dt)))

    def ps(self, es, shape, dt, nm="ps"):
        return T(es.enter_context(self.nc.psum_tensor(self.name(nm), list(shape), dt)))

    def _waits(self, eng, reads, writes):
        need = {}
        for b in reads:
            if b.w is not None:
                s, v = b.w
                if need.get(s, 0) < v:
                    need[s] = v
        for b in writes:
            if b.w is not None:
                s, v = b.w
                if need.get(s, 0) < v:
                    need[s] = v
            for s, v in b.r.items():
                if need.get(s, 0) < v:
                    need[s] = v
        for s, v in need.items():
            if eng.seen.get(s, 0) < v:
                eng.obj.wait_ge(s, v)
                eng.seen[s] = v

    def _post(self, s, v, reads, writes):
        for b in reads:
            if b.r.get(s, 0) < v:
                b.r[s] = v
        for b in writes:
            b.w = (s, v)
            b.r = {}

    def op(self, eng, emit, reads=(), writes=()):
        reads = [x.b if isinstance(x, T) else x for x in reads]
        writes = [x.b if isinstance(x, T) else x for x in writes]
        self._waits(eng, reads, writes)
        ins = emit(eng.obj)
        eng.count += 1
        ins.then_inc(eng.sem, 1)
        self._post(eng.sem, eng.count, reads, writes)

    def mm(self, out, pairs, reads=(), writes=(), start=True, stop=True):
        n = len(pairs)

        def emit(e):
            ins = None
            for i, (l, r) in enumerate(pairs):
                ins = e.matmul(out, lhsT=l, rhs=r, start=(start and i == 0), stop=(stop and i == n - 1))
            return ins
        self.op(self.pe, emit, reads, writes)

    def dma(self, eng, out, in_, reads=(), writes=()):
        reads = [x.b if isinstance(x, T) else x for x in reads]
        writes = [x.b if isinstance(x, T) else x for x in writes]
        self._waits(eng, reads, writes)
        i = self.dma_next
        self.dma_next = (i + 1) % len(self.dma_sems)
        s = self.dma_sems[i]
        if self.dma_val[i] > 0 and eng.seen.get(s, 0) < self.dma_val[i]:
            eng.obj.wait_ge(s, self.dma_val[i])
            eng.seen[s] = self.dma_val[i]
        self.dma_val[i] += 16
        eng.obj.dma_start(out=out, in_=in_).then_inc(s, 16)
        self._post(s, self.dma_val[i], reads, writes)

    def barrier(self):
        for e in self.engs:
            for f in self.engs:
                if f is not e and f.count > 0 and e.seen.get(f.sem, 0) < f.count:
                    e.obj.wait_ge(f.sem, f.count)
                    e.seen[f.sem] = f.count
            for i, s in enumerate(self.dma_sems):
                if self.dma_val[i] > 0 and e.seen.get(s, 0) < self.dma_val[i]:
                    e.obj.wait_ge(s, self.dma_val[i])
                    e.seen[s] = self.dma_val[i]


def u_part(g, c):
    gp = g // 2
    return gp // 3, 32 * (gp % 3) + 16 * (g % 2) + c


def host_consts(L):
    f32 = np.float32
    c = {}
    pos = np.arange(L, dtype=f32)
    inv = (1.0 / (10000.0 ** (np.arange(0, 64, 2, dtype=f32) / 64.0))).astype(f32)
    ang = (pos[:, None] * inv[None, :]).astype(f32)
    cos = np.cos(ang).astype(f32)
    sin = np.sin(ang).astype(f32)
    cos64 = np.concatenate([cos, cos], 1)
    sin64 = np.concatenate([-sin, sin], 1)
    c["cosT"] = np.ascontiguousarray(np.tile(cos64.T, (2, 1)))
    c["sinT"] = np.ascontiguousarray(np.tile(sin64.T, (2, 1)))
    c["cosT8"] = (c["cosT"] * f32(0.125)).astype(f32)
    c["sinT8"] = (c["sinT"] * f32(0.125)).astype(f32)
    c["cosM"] = np.ascontiguousarray(np.tile(cos64, (1, 4)))
    c["sinM"] = np.ascontiguousarray(np.tile(sin64, (1, 4)))
    h = np.arange(4, dtype=np.float64)
    log_g = np.log1p(-np.exp2(-5.0 - h))
    idx = np.arange(128, dtype=np.float64)
    dmat = np.exp(log_g[:, None, None] * np.abs(idx[:, None] - idx[None, :]))
    c["dmat"] = np.ascontiguousarray(dmat.transpose(1, 0, 2).reshape(128, 512)).astype(f32)
    w_f = np.exp(log_g[None, :] * (127.0 - idx)[:, None])
    w_b = np.exp(log_g[None, :] * idx[:, None])
    c["wft"] = np.repeat(w_f, 64, axis=1).astype(f32)
    c["wbt"] = np.repeat(w_b, 64, axis=1).astype(f32)
    q_f = np.exp(log_g[None, :] * (idx + 1.0)[:, None])
    q_b = np.exp(log_g[None, :] * (128.0 - idx)[:, None])
    qft = np.zeros((128, 2, 128))
    qbt = np.zeros((128, 2, 128))
    decr = np.zeros((128, 2))
    for hp in range(2):
        for hh in range(2):
            hd = 2 * hp + hh
            qft[hh * 64:(hh + 1) * 64, hp, :] = q_f[:, hd][None, :]
            qbt[hh * 64:(hh + 1) * 64, hp, :] = q_b[:, hd][None, :]
            decr[hh * 64:(hh + 1) * 64, hp] = np.exp(log_g[hd] * 128.0)
    c["qft"] = qft.astype(f32)
    c["qbt"] = qbt.astype(f32)
    c["decr"] = decr.astype(f32)
    j = np.arange(128)[:, None]
    i = np.arange(128)[None, :]
    c["m_le"] = (j <= i).astype(f32)
    c["m_ge"] = (j >= i).astype(f32)
    c["m_gt"] = (j > i).astype(f32)
    c["m_lt"] = (j < i).astype(f32)
    c["m_le4"] = np.tile(c["m_le"], (1, 4))
    c["m_gt4"] = np.tile(c["m_gt"], (1, 4))
    c["ident"] = np.eye(128, dtype=f32)
    bd = np.zeros((128, 128), f32)
    bd[0:64, 0:64] = 1
    bd[64:128, 64:128] = 1
    c["bd_r"] = np.tile(bd, (1, 2))
    bdg = np.zeros((128, 256), f32)
    bdg[0:64, 0:128] = 1
    bdg[64:128, 128:256] = 1
    c["bd_g"] = np.tile(bdg, (1, 2))
    mb = np.zeros((128, 128), f32)
    for p in range(96):
        gg = (p % 32) // 16
        mb[p, gg * 64:(gg + 1) * 64] = 1
    c["maskb"] = mb
    return c


def host_layer_params(inp, depth):
    f32 = np.float32
    o = {}
    w_in = inp["w_in"][:depth]
    rq, rk, rv, rg = w_in[:, :, 0:256], w_in[:, :, 256:512], w_in[:, :, 512:768], w_in[:, :, 768:1024]
    u = w_in[:, :, 1024:1280]
    gq, gk, gv = w_in[:, :, 1280:1536], w_in[:, :, 1536:1792], w_in[:, :, 1792:2304]
    lr, gg = w_in[:, :, 2304:2320], w_in[:, :, 2320:2832]
    perm = np.concatenate([(np.arange(64) + 32) % 64 + 64 * hd for hd in range(4)])
    rqp, rkp = rq[:, :, perm], rk[:, :, perm]
    up = np.zeros((depth, D, 384), f32)
    for g in range(16):
        for cc in range(16):
            ti, p = u_part(g, cc)
            up[:, :, ti * 128 + p] = u[:, :, g * 16 + cc]
    lrp = np.zeros((depth, D, 32), f32)
    lrp[:, :, :16] = lr
    fm = [rq[:, :, 0:128], rqp[:, :, 0:128], rq[:, :, 128:256], rqp[:, :, 128:256],
          rk[:, :, 0:128], rkp[:, :, 0:128], rk[:, :, 128:256], rkp[:, :, 128:256],
          gq, gk, up, lrp]
    tm = [rk, rkp, rv, rg, gv, gg, gk]
    o["w_in2"] = np.ascontiguousarray(np.concatenate(fm + tm, axis=2))
    assert o["w_in2"].shape[2] == NIN2
    wb = inp["w_branch_b"][:depth]
    wb2 = np.zeros((depth, 384, 2 * D), f32)
    for g in range(16):
        for cc in range(16):
            ti, p = u_part(g, cc)
            wb2[:, ti * 128 + p, :] = wb[:, g * 16 + cc, :]
    o["wb2"] = wb2
    o["wa"] = np.ascontiguousarray(inp["w_branch_a"][:depth])
    o["wc"] = np.ascontiguousarray(inp["w_branch_c"][:depth])
    o["wm"] = np.ascontiguousarray(inp["w_merge_gate"][:depth])
    o["wo"] = np.ascontiguousarray(inp["w_out"][:depth])
    o["wf1"] = np.ascontiguousarray(inp["w_ff1"][:depth])
    o["wf2"] = np.ascontiguousarray(inp["w_ff2"][:depth])
    wg = np.zeros((depth, 32, 512), f32)
    wg[:, 0:16, 0:256] = inp["gla_w_gate"][:depth, 0]
    wg[:, 0:16, 256:512] = inp["gla_w_gate"][:depth, 1]
    wg[:, 16, 0:256] = inp["gla_b_gate"][:depth, 0]
    wg[:, 16, 256:512] = inp["gla_b_gate"][:depth, 1]
    o["wg"] = wg

    def pk(v, k):
        return np.ascontiguousarray(v.reshape(v.shape[0], k, 128).transpose(0, 2, 1))
    o["g1"] = pk(inp["norm1_g"][:depth], 8)
    o["g2"] = pk(inp["norm2_g"][:depth], 8)
    o["gfin"] = pk(inp["final_norm_g"][None, :], 8)[0]
    o["bm"] = pk(inp["b_merge_gate"][:depth], 24)
    o["rng"] = np.ascontiguousarray(np.broadcast_to(inp["ret_norm_g"][:depth, None, :], (depth, 128, 256)))
    o["gng"] = np.ascontiguousarray(np.broadcast_to(inp["gla_norm_g"][:depth, None, :], (depth, 128, 512)))
    lre, lim, ldt = inp["s5_lam_re"][:depth], inp["s5_lam_im"][:depth], inp["s5_log_dt"][:depth]
    bre, bim = inp["s5_b_re"][:depth], inp["s5_b_im"][:depth]
    cre, cim = inp["s5_c_re"][:depth], inp["s5_c_im"][:depth]
    lamc = np.zeros((depth, 128, 3, 16), f32)
    bc = np.zeros((depth, 128, 2, 16, 16), f32)
    ccm = np.zeros((depth, 128, 2, 16, 16), f32)
    for gp in range(8):
        for dr in range(2):
            un = gp * 2 + dr
            for g2 in range(2):
                g = 2 * gp + g2
                sl = slice(g2 * 64, (g2 + 1) * 64)
                lamc[:, sl, 0, un] = lre[:, dr, g, :]
                lamc[:, sl, 1, un] = lim[:, dr, g, :]
                lamc[:, sl, 2, un] = ldt[:, dr, g][:, None]
                bc[:, sl, 0, un, :] = bre[:, dr, g]
                bc[:, sl, 1, un, :] = bim[:, dr, g]
                ccm[:, sl, 0, un, :] = cre[:, dr, g].transpose(0, 2, 1)
                ccm[:, sl, 1, un, :] = cim[:, dr, g].transpose(0, 2, 1)
    o["lamc"], o["bc"], o["cc"] = lamc, bc, ccm
    lamb = np.zeros((depth, 128, 3, 3, 2, 64), f32)
    lamb[:, :, 0] = -0.5
    lamb[:, :, 1] = 1.0
    lamb[:, :, 2] = -3.0
    bb = np.zeros((depth, 128, 2, 3, 2, 64), f32)
    du = np.zeros((depth, 128, 3), f32)
    sd = inp["s5_d"][:depth]
    for g in range(16):
        for cch in range(16):
            ti, p = u_part(g, cch)
            du[:, p, ti] = sd[:, g * 16 + cch]
            for dr in range(2):
                lamb[:, p, 0, ti, dr, :] = lre[:, dr, g, :]
                lamb[:, p, 1, ti, dr, :] = lim[:, dr, g, :]
                lamb[:, p, 2, ti, dr, :] = ldt[:, dr, g][:, None]
                bb[:, p, 0, ti, dr, :] = bre[:, dr, g, :, cch]
                bb[:, p, 1, ti, dr, :] = bim[:, dr, g, :, cch]
    o["lamb"], o["bb"], o["du"] = lamb, bb, du
    return o


CONST_SHAPES = None


def build(L, DEPTH, NSEQ, consts, lp):
    nc = bass.Bass("TRN2", target_bir_lowering=False)
    NCH = L // 128
    NT = L // 512
    NB = L // T1

    def din(name, shape, dt=F32):
        return nc.dram_tensor(name, list(shape), dt, kind="ExternalInput").ap()

    def dscr(name, shape, dt):
        return nc.dram_tensor(name, list(shape), dt, kind="Internal").ap()

    x_in = din("x", [NSEQ, L, D])
    out_d = nc.dram_tensor("out", [NSEQ, L, D], F32, kind="ExternalOutput").ap()
    cd = {k: din("c_" + k, v.shape) for k, v in consts.items()}
    pd = {k: din("p_" + k, v.shape) for k, v in lp.items()}
    wbf = {k: dscr("wbf_" + k, lp[k].shape, BF16) for k in ["w_in2", "wb2", "wa", "wc", "wm", "wo", "wf1", "wf2", "wg"]}
    XT = dscr("XT", [NSEQ, 8, 128, L], F32)
    HS = dscr("HS", [8, 128, L], BF16)
    FMS = {k: dscr("S_" + k, [2, 128, L], BF16) for k in ["rq", "rkt", "gqf", "gkf", "gqb", "gkb"]}
    UTS = dscr("S_ut", [3, 128, L], BF16)
    YTS = dscr("S_yt", [3, 128, L], BF16)
    RV = dscr("S_rv", [L, 256], BF16)
    RGS = dscr("S_rgs", [L, 256], BF16)
    GV = dscr("S_gv", [L, 512], BF16)
    GGS = dscr("S_ggs", [L, 512], BF16)
    KVR = dscr("S_kvr", [NCH, 128, 512], F32)
    KVG = dscr("S_kvg", [NCH, 128, 1024], F32)
    SR = dscr("S_sr", [NCH, 128, 2, 256], BF16)
    SG = dscr("S_sg", [NCH, 128, 2, 512], BF16)

    with ExitStack() as es0:
        P = Prog(nc, es0)
        sp, act, pool, dve, pe = P.sp, P.act, P.pool, P.dve, P.pe

        def cload(key, dt=F32, shape=None):
            src = cd[key]
            shp = list(consts[key].shape)
            t = P.sb(es0, shp, F32, "c_" + key)
            P.dma(sp, t.t[:], src, writes=[t])
            if dt == BF16:
                tb = P.sb(es0, shp, BF16, "cb_" + key)
                P.op(dve, lambda e: e.tensor_copy(out=tb.t[:], in_=t.t[:]), reads=[t], writes=[tb])
                return tb
            return t

        ident_f = cload("ident")
        ident_b = P.sb(es0, [128, 128], BF16, "identb")
        P.op(dve, lambda e: e.tensor_copy(out=ident_b.t[:], in_=ident_f.t[:]), reads=[ident_f], writes=[ident_b])
        ones_b = P.sb(es0, [128, 128], BF16, "onesb")
        P.op(pool, lambda e: e.memset(ones_b.t[:], 1.0), writes=[ones_b])
        eps_t = P.sb(es0, [128, 1], F32, "eps")
        P.op(pool, lambda e: e.memset(eps_t.t[:], EPS), writes=[eps_t])
        one_t = P.sb(es0, [128, 1], F32, "one")
        P.op(pool, lambda e: e.memset(one_t.t[:], 1.0), writes=[one_t])
        gfin = P.sb(es0, [128, 8], F32, "gfin")
        P.dma(sp, gfin.t[:], pd["gfin"], writes=[gfin])

        psf = RR([P.ps(es0, [128, 512], F32, "psf") for _ in range(6)])
        psb = RR([P.ps(es0, [128, 1024], BF16, "psb") for _ in range(2)])

        with ExitStack() as es:
            wst = RR([P.sb(es, [128, 2048], F32, "wst") for _ in range(3)])
            wsb = RR([P.sb(es, [128, 2048], BF16, "wsb") for _ in range(3)])
            cnt = 0
            for k in ([] if NOCAST else wbf):
                shp = lp[k].shape
                for l in range(DEPTH):
                    rows, cols = shp[1], shp[2]
                    for r0 in range(0, rows, 128):
                        r1 = min(rows, r0 + 128)
                        for c0 in range(0, cols, 2048):
                            c1 = min(cols, c0 + 2048)
                            a = wst.get()
                            b = wsb.get()
                            P.dma(sp, a.t[0:r1 - r0, 0:c1 - c0], pd[k][l, r0:r1, c0:c1], writes=[a])
                            eng = [act, dve, pool][cnt % 3]
                            cnt += 1
                            if eng is act:
                                P.op(act, lambda e: e.activation(out=b.t[0:r1 - r0, 0:c1 - c0], in_=a.t[0:r1 - r0, 0:c1 - c0], func=AF.Copy),
                                     reads=[a], writes=[b])
                            else:
                                P.op(eng, lambda e: e.tensor_copy(out=b.t[0:r1 - r0, 0:c1 - c0], in_=a.t[0:r1 - r0, 0:c1 - c0]),
                                     reads=[a], writes=[b])
                            P.dma(sp, wbf[k][l, r0:r1, c0:c1], b.t[0:r1 - r0, 0:c1 - c0], reads=[b])
            P.barrier()

        def rmsnorm(es, xt, gt, outT, sq, rstd):
            P.op(act, lambda e: e.activation(out=sq.t[:], in_=xt.t[:], func=AF.Square), reads=[xt], writes=[sq])
            ps = psf.get()
            P.mm(ps.t[:, :], [(ones_b.t[:], sq.t[:, k, :]) for k in range(8)], reads=[ones_b, sq], writes=[ps])
            P.op(act, lambda e: e.activation(out=rstd.t[:], in_=ps.t[:], func=AF.Sqrt, scale=1.0 / D, bias=eps_t.t[:, 0:1]),
                 reads=[ps, eps_t], writes=[rstd])
            P.op(dve, lambda e: e.reciprocal(out=rstd.t[:], in_=rstd.t[:]), reads=[rstd], writes=[rstd])
            for k in range(8):
                P.op(dve, lambda e, k=k: e.scalar_tensor_tensor(out=outT.t[:, k, :], in0=xt.t[:, k, :], scalar=gt.t[:, k:k + 1],
                                                                in1=rstd.t[:], op0=ALU.mult, op1=ALU.mult),
                     reads=[xt, rstd, gt], writes=[outT])

        def wview(key, l):
            return wbf[key][l].rearrange("(k p) c -> p k c", p=128)

        with ExitStack() as es:
            xin = RR([P.sb(es, [128, D], F32, "xin") for _ in range(2)])
            xst = RR([P.sb(es, [128, 8, 128], F32, "xst") for _ in range(2)])
            for s in range(NSEQ):
                for c in range(NCH):
                    xi = xin.get()
                    P.dma(sp, xi.t[:], x_in[s, c * 128:(c + 1) * 128, :], writes=[xi])
                    xs = xst.get()
                    for half in range(2):
                        ps = psf.get()
                        for kk in range(4):
                            k = half * 4 + kk
                            P.op(pe, lambda e, k=k, kk=kk: e.transpose(out=ps.t[:, kk * 128:(kk + 1) * 128],
                                                                         in_=xi.t[:, k * 128:(k + 1) * 128], identity=ident_f.t[:]),
                                 reads=[xi, ident_f], writes=[ps])
                        P.op(act if half == 0 else dve,
                             (lambda e, half=half: e.activation(out=xs.t[:, half * 4:half * 4 + 4, :], in_=ps.t[:].rearrange("p (k t) -> p k t", k=4), func=AF.Copy))
                             if half == 0 else
                             (lambda e, half=half: e.tensor_copy(out=xs.t[:, half * 4:half * 4 + 4, :], in_=ps.t[:].rearrange("p (k t) -> p k t", k=4))),
                             reads=[ps], writes=[xs])
                    P.dma(sp, XT[s, :, :, c * 128:(c + 1) * 128].rearrange("k p t -> p k t"), xs.t[:], reads=[xs])
            P.barrier()

        wft = cload("wft"); wbt = cload("wbt"); dmat = cload("dmat"); qft = cload("qft"); qbt = cload("qbt"); decr = cload("decr")
        m_le = cload("m_le"); m_ge = cload("m_ge"); m_gt = cload("m_gt"); m_lt = cload("m_lt")
        m_le4 = cload("m_le4"); m_gt4 = cload("m_gt4"); bd_r = cload("bd_r"); bd_g = cload("bd_g")
        maskb = cload("maskb")
        decg = P.sb(es0, [128, NCH, 2, 2], F32, "decg")

        def phase_B2(l, s):
            PI = math.pi
            PB = Buf()

            def V(fn, r=(), w=()):
                P.op(dve, fn, reads=[PB] + list(r), writes=[PB] + list(w))

            def A_(fn):
                P.op(act, fn, reads=[PB], writes=[PB])

            def tt(o, a, b, op):
                V(lambda e: e.tensor_tensor(out=o, in0=a, in1=b, op=op))

            def ts(o, a, s1, op0, s2=None, op1=None):
                if op1 is None:
                    V(lambda e: e.tensor_scalar(out=o, in0=a, scalar1=s1, scalar2=None, op0=op0))
                else:
                    V(lambda e: e.tensor_scalar(out=o, in0=a, scalar1=s1, scalar2=s2, op0=op0, op1=op1))

            def cp(o, a):
                V(lambda e: e.tensor_copy(out=o, in_=a))

            def cmul(ore, oim, are, aim, bre, bim, t1, t2):
                tt(t1, are, bre, ALU.mult)
                tt(t2, aim, bim, ALU.mult)
                tt(ore, t1, t2, ALU.subtract)
                tt(t1, are, bim, ALU.mult)
                tt(t2, aim, bre, ALU.mult)
                tt(oim, t1, t2, ALU.add)

            with ExitStack() as esB:
                WZ = P.sb(esB, [128, 3, 2, 8, 2, 128], BF16, "WZ")
                WY = P.sb(esB, [128, 16, 8, 2, 32], BF16, "WY")
                TAPS = P.sb(esB, [128, 3, 15, 128], BF16, "TAPS")
                A8 = P.sb(esB, [128, 4, 16], F32, "A8")
                for tl in (WZ, WY, TAPS):
                    P.op(pool, lambda e, tl=tl: e.memset(tl.t[:], 0.0), writes=[tl, PB])
                with ExitStack() as es:
                    def lam_stuff(lam3, n, tag):
                        W = P.sb(es, [128, 16, n], F32, "W" + tag)
                        Wi = P.sb(es, [128, n], I32, "Wi" + tag)
                        w = lambda i: W.t[:, i, :]
                        LRE, LIM, LDT = lam3[:, 0, :], lam3[:, 1, :], lam3[:, 2, :]
                        A_(lambda e: e.activation(out=w(0), in_=LDT, func=AF.Exp))
                        tt(w(1), LRE, w(0), ALU.mult)
                        tt(w(2), LIM, w(0), ALU.mult)
                        A_(lambda e: e.activation(out=w(3), in_=w(1), func=AF.Exp))
                        ts(w(4), w(2), 1.0 / (2 * PI), ALU.mult)
                        cp(Wi.t[:], w(4))
                        cp(w(4), Wi.t[:])
                        V(lambda e: e.scalar_tensor_tensor(out=w(2), in0=w(4), scalar=-2 * PI, in1=w(2), op0=ALU.mult, op1=ALU.add))
                        for _ in range(2):
                            ts(w(4), w(2), PI, ALU.is_gt, -2 * PI, ALU.mult)
                            tt(w(2), w(2), w(4), ALU.add)
                            ts(w(4), w(2), -PI, ALU.is_lt, 2 * PI, ALU.mult)
                            tt(w(2), w(2), w(4), ALU.add)
                        ts(w(5), w(2), PI / 2, ALU.add)
                        ts(w(4), w(5), PI, ALU.is_gt, -2 * PI, ALU.mult)
                        tt(w(5), w(5), w(4), ALU.add)
                        ts(w(2), w(2), 3.14159, ALU.min, -3.14159, ALU.max)
                        ts(w(5), w(5), 3.14159, ALU.min, -3.14159, ALU.max)
                        A_(lambda e: e.activation(out=w(6), in_=w(2), func=AF.Sin))
                        A_(lambda e: e.activation(out=w(7), in_=w(5), func=AF.Sin))
                        tt(w(8), w(3), w(7), ALU.mult)
                        tt(w(9), w(3), w(6), ALU.mult)
                        ts(w(12), w(8), -1.0, ALU.add)
                        tt(w(13), LRE, LRE, ALU.mult)
                        tt(w(14), LIM, LIM, ALU.mult)
                        tt(w(13), w(13), w(14), ALU.add)
                        V(lambda e: e.reciprocal(out=w(13), in_=w(13)))
                        tt(w(14), w(12), LRE, ALU.mult)
                        tt(w(15), w(9), LIM, ALU.mult)
                        tt(w(14), w(14), w(15), ALU.add)
                        tt(w(10), w(14), w(13), ALU.mult)
                        tt(w(14), w(9), LRE, ALU.mult)
                        tt(w(15), w(12), LIM, ALU.mult)
                        tt(w(14), w(14), w(15), ALU.subtract)
                        tt(w(11), w(14), w(13), ALU.mult)
                        return W

                    def powers(W, n, npow, tag):
                        Pw = P.sb(es, [128, npow + 1, 2, n], F32, "Pw" + tag)
                        V(lambda e: e.memset(Pw.t[:, 0, 0, :], 1.0))
                        V(lambda e: e.memset(Pw.t[:, 0, 1, :], 0.0))
                        for m in range(npow):
                            cmul(Pw.t[:, m + 1, 0, :], Pw.t[:, m + 1, 1, :], Pw.t[:, m, 0, :], Pw.t[:, m, 1, :],
                                 W.t[:, 8, :], W.t[:, 9, :], W.t[:, 14, :], W.t[:, 15, :])
                        return Pw

                    lamc = P.sb(es, [128, 3, 16], F32, "lamc")
                    bct = P.sb(es, [128, 2, 16, 16], F32, "bct")
                    cct = P.sb(es, [128, 2, 16, 16], F32, "cct")
                    P.dma(sp, lamc.t[:], pd["lamc"][l], writes=[lamc, PB])
                    P.dma(sp, bct.t[:], pd["bc"][l], writes=[bct, PB])
                    P.dma(sp, cct.t[:], pd["cc"][l], writes=[cct, PB])
                    Wc = lam_stuff(lamc.t, 16, "c")
                    Pc = powers(Wc, 16, 8, "c")
                    A_(lambda e: e.activation(out=A8.t[:, 0, :], in_=Wc.t[:, 1, :], func=AF.Exp, scale=8.0))
                    A_(lambda e: e.activation(out=A8.t[:, 3, :], in_=Wc.t[:, 1, :], func=AF.Exp, scale=-8.0))
                    tt(A8.t[:, 1, :], Pc.t[:, 8, 0, :], A8.t[:, 3, :], ALU.mult)
                    tt(A8.t[:, 2, :], Pc.t[:, 8, 1, :], A8.t[:, 3, :], ALU.mult)

                    def bc16(ap2):
                        return ap2.unsqueeze(2).broadcast_to([128, 16, 16])
                    T3 = P.sb(es, [128, 6, 16, 16], F32, "T3")
                    Bb = P.sb(es, [128, 2, 16, 16], F32, "Bb")
                    cmul(Bb.t[:, 0], Bb.t[:, 1], bct.t[:, 0], bct.t[:, 1], bc16(Wc.t[:, 10, :]), bc16(Wc.t[:, 11, :]), T3.t[:, 0], T3.t[:, 1])
                    CD = P.sb(es, [128, 16, 2, 32], F32, "CD")
                    BPD = P.sb(es, [128, 16, 2, 32], F32, "BPD")
                    V(lambda e: e.memset(CD.t[:], 0.0))
                    V(lambda e: e.memset(BPD.t[:], 0.0))
                    cp(CD.t[0:64, :, 0, 0:16], cct.t[0:64, 0])
                    cp(CD.t[64:128, :, 0, 16:32], cct.t[64:128, 0])
                    ts(CD.t[0:64, :, 1, 0:16], cct.t[0:64, 1], -1.0, ALU.mult)
                    ts(CD.t[64:128, :, 1, 16:32], cct.t[64:128, 1], -1.0, ALU.mult)
                    dut = P.sb(es, [128, 3], F32, "dut")
                    P.dma(sp, dut.t[:], pd["du"][l], writes=[dut, PB])
                    DD = P.sb(es, [128, 3, 128], F32, "DD")
                    for ti in range(3):
                        ts(DD.t[:, ti, :], ident_f.t[:], dut.t[:, ti:ti + 1], ALU.mult)
                    pst = [psf.get() for _ in range(3)]
                    for tau in range(8):
                        cmul(T3.t[:, 2], T3.t[:, 3], Bb.t[:, 0], Bb.t[:, 1], bc16(Pc.t[:, tau, 0, :]), bc16(Pc.t[:, tau, 1, :]), T3.t[:, 0], T3.t[:, 1])
                        for ri in range(2):
                            cp(BPD.t[0:64, :, ri, 0:16], T3.t[0:64, 2 + ri])
                            cp(BPD.t[64:128, :, ri, 16:32], T3.t[64:128, 2 + ri])
                        for gp in range(8):
                            ti, sl = gp // 3, gp % 3
                            rs = slice(32 * sl, 32 * sl + 32)
                            for dr in range(2):
                                un = gp * 2 + dr
                                k = 7 + tau if dr == 0 else 7 - tau
                                if tau == 0 and dr == 1:
                                    continue
                                pairs = []
                                if tau == 0:
                                    pairs.append((DD.t[rs, ti, 32 * sl:32 * sl + 32], ident_f.t[rs, 32 * sl:32 * sl + 32]))
                                    for d2 in range(2):
                                        u2 = gp * 2 + d2
                                        pairs += [(BPD.t[:, u2, 0, :], CD.t[:, u2, 0, :]), (BPD.t[:, u2, 1, :], CD.t[:, u2, 1, :])]
                                else:
                                    pairs = [(BPD.t[:, un, 0, :], CD.t[:, un, 0, :]), (BPD.t[:, un, 1, :], CD.t[:, un, 1, :])]
                                P.mm(pst[ti].t[rs, k * 32:(k + 1) * 32], pairs, reads=[PB, ident_f], writes=[pst[ti]])
                    for gp in range(8):
                        ti, sl = gp // 3, gp % 3
                        rs = slice(32 * sl, 32 * sl + 32)
                        V(lambda e: e.tensor_copy(out=TAPS.t[rs, ti, :, 32 * sl:32 * sl + 32],
                                                  in_=pst[ti].t[rs, 0:480].rearrange("p (k c) -> p k c", c=32)), r=[pst[ti]], w=[TAPS])
                    PWi = P.sb(es, [128, 2, 16], F32, "PWi")
                    for i in range(8):
                        for ri in range(2):
                            cp(PWi.t[:, ri, :].rearrange("p (g d) -> p g d", d=2)[:, :, 0], Pc.t[:, i + 1, ri, :].rearrange("p (g d) -> p g d", d=2)[:, :, 0])
                            cp(PWi.t[:, ri, :].rearrange("p (g d) -> p g d", d=2)[:, :, 1], Pc.t[:, 8 - i, ri, :].rearrange("p (g d) -> p g d", d=2)[:, :, 1])
                        cmul(T3.t[:, 2], T3.t[:, 3], cct.t[:, 0], cct.t[:, 1], bc16(PWi.t[:, 0, :]), bc16(PWi.t[:, 1, :]), T3.t[:, 0], T3.t[:, 1])
                        V(lambda e: e.tensor_copy(out=WY.t[0:64, :, i, 0, 0:16], in_=T3.t[0:64, 2]), w=[WY])
                        V(lambda e: e.tensor_copy(out=WY.t[64:128, :, i, 0, 16:32], in_=T3.t[64:128, 2]), w=[WY])
                        V(lambda e: e.tensor_scalar(out=WY.t[0:64, :, i, 1, 0:16], in0=T3.t[0:64, 3], scalar1=-1.0, scalar2=None, op0=ALU.mult), w=[WY])
                        V(lambda e: e.tensor_scalar(out=WY.t[64:128, :, i, 1, 16:32], in0=T3.t[64:128, 3], scalar1=-1.0, scalar2=None, op0=ALU.mult), w=[WY])
                    lamb = P.sb(es, [128, 3, 384], F32, "lamb")
                    bbt = P.sb(es, [128, 2, 384], F32, "bbt")
                    P.dma(sp, lamb.t[:], pd["lamb"][l].rearrange("p w t d q -> p w (t d q)"), writes=[lamb, PB])
                    P.dma(sp, bbt.t[:], pd["bb"][l].rearrange("p w t d q -> p w (t d q)"), writes=[bbt, PB])
                    Wb = lam_stuff(lamb.t, 384, "b")
                    Pb = powers(Wb, 384, 7, "b")
                    Bbb = P.sb(es, [128, 2, 384], F32, "Bbb")
                    T4 = P.sb(es, [128, 6, 384], F32, "T4")
                    cmul(Bbb.t[:, 0], Bbb.t[:, 1], bbt.t[:, 0], bbt.t[:, 1], Wb.t[:, 10, :], Wb.t[:, 11, :], T4.t[:, 0], T4.t[:, 1])

                    def tdq(ap2):
                        return ap2.rearrange("p (t d q) -> p t d q", t=3, d=2)
                    for j in range(8):
                        for ri in range(2):
                            cp(tdq(T4.t[:, 2 + ri, :])[:, :, 0, :], tdq(Pb.t[:, 7 - j, ri, :])[:, :, 0, :])
                            cp(tdq(T4.t[:, 2 + ri, :])[:, :, 1, :], tdq(Pb.t[:, j, ri, :])[:, :, 1, :])
                        cmul(T4.t[:, 4], T4.t[:, 5], T4.t[:, 2], T4.t[:, 3], Bbb.t[:, 0], Bbb.t[:, 1], T4.t[:, 0], T4.t[:, 1])
                        for ri in range(2):
                            for g2 in range(2):
                                mk = maskb.t[:, g2 * 64:(g2 + 1) * 64].unsqueeze(1).unsqueeze(1).broadcast_to([128, 3, 2, 64])
                                V(lambda e: e.tensor_tensor(out=WZ.t[:, :, :, j, ri, g2 * 64:(g2 + 1) * 64], in0=tdq(T4.t[:, 4 + ri, :]), in1=mk, op=ALU.mult),
                                  r=[maskb], w=[WZ])
                    P.barrier()
                NBp = NB + 1
                uTs = P.sb(esB, [128, 3, L], BF16, "uTs")
                P.dma(sp, uTs.t[:], UTS.rearrange("k p t -> p k t"), writes=[uTs])
                Xbf = P.sb(esB, [128, 16, 2, NB], BF16, "Xbf")
                with ExitStack() as es:
                    zb = [P.sb(es, [128, 4, NB], F32, "zb%d" % i) for i in range(4)]
                    Er = P.sb(es, [128, 4, NBp], F32, "Er")
                    Ei = P.sb(es, [128, 4, NBp], F32, "Ei")
                    RHO = P.sb(es, [128, 4, NB], F32, "RHO")
                    Uw = P.sb(es, [128, 2, 2, 4], F32, "Uw")
                    Ut = P.sb(es, [128, 2, 4], F32, "Ut")
                    for dr in range(2):
                        for hf in range(2):
                            uns = [(4 * hf + g) * 2 + dr for g in range(4)]
                            usl = A8.t[:, :, :].rearrange("p w (g d) -> p w g d", d=2)[:, :, 4 * hf:4 * hf + 4, dr]
                            V(lambda e: e.memset(Er.t[:, :, 0:1], 1.0))
                            V(lambda e: e.memset(Ei.t[:, :, 0:1], 0.0))
                            cp(Uw.t[:, 0, 0, :], usl[:, 1, :])
                            cp(Uw.t[:, 0, 1, :], usl[:, 2, :])
                            k = 0
                            cur = 0
                            while (1 << k) <= NB:
                                lo = 1 << k
                                hi = min(lo * 2, NBp)
                                wd = hi - lo
                                ub_r = Uw.t[:, cur, 0, :].unsqueeze(2).broadcast_to([128, 4, wd])
                                ub_i = Uw.t[:, cur, 1, :].unsqueeze(2).broadcast_to([128, 4, wd])
                                cmul(Er.t[:, :, lo:hi], Ei.t[:, :, lo:hi], Er.t[:, :, 0:wd], Ei.t[:, :, 0:wd], ub_r, ub_i, zb[0].t[:, :, 0:wd], zb[1].t[:, :, 0:wd])
                                cmul(Uw.t[:, 1 - cur, 0, :], Uw.t[:, 1 - cur, 1, :], Uw.t[:, cur, 0, :], Uw.t[:, cur, 1, :],
                                     Uw.t[:, cur, 0, :], Uw.t[:, cur, 1, :], Ut.t[:, 0, :], Ut.t[:, 1, :])
                                cur = 1 - cur
                                k += 1
                            cp(RHO.t[:], usl[:, 0, :].unsqueeze(2).broadcast_to([128, 4, NB]))
                            for g in range(4):
                                gp = 4 * hf + g
                                ti, sl = gp // 3, gp % 3
                                rs = slice(32 * sl, 32 * sl + 32)
                                for ri in range(2):
                                    ps = psf.get()
                                    pairs = []
                                    for j in range(8):
                                        if dr == 0:
                                            rhs = uTs.t[rs, ti, j:L:8]
                                        else:
                                            rhs = uTs.t[rs, ti, j:L:8][:, ::-1]
                                        pairs.append((WZ.t[rs, ti, dr, j, ri, :], rhs))
                                    P.mm(ps.t[:, 0:NB], pairs, reads=[WZ, uTs], writes=[ps])
                                    if ri == 0:
                                        P.op(act, lambda e: e.activation(out=zb[0].t[:, g, :], in_=ps.t[:, 0:NB], func=AF.Copy), reads=[ps, PB], writes=[PB])
                                    else:
                                        V(lambda e: e.tensor_copy(out=zb[1].t[:, g, :], in_=ps.t[:, 0:NB]), r=[ps])
                            zr, zi, t1, t2 = zb[0].t, zb[1].t, zb[2].t, zb[3].t
                            c1, s1 = Er.t[:, :, 1:NBp], Ei.t[:, :, 1:NBp]
                            tt(t1[:], zr[:], c1, ALU.mult)
                            tt(t2[:], zi[:], s1, ALU.mult)
                            tt(t1[:], t1[:], t2[:], ALU.add)
                            tt(t2[:], zr[:], s1, ALU.mult)
                            tt(zr[:], zi[:], c1, ALU.mult)
                            tt(zr[:], zr[:], t2[:], ALU.subtract)
                            for g in range(4):
                                V(lambda e: e.tensor_tensor_scan(out=zi[:, g, :], data0=RHO.t[:, g, :], data1=t1[:, g, :], initial=0.0, op0=ALU.mult, op1=ALU.add))
                                V(lambda e: e.tensor_tensor_scan(out=t2[:, g, :], data0=RHO.t[:, g, :], data1=zr[:, g, :], initial=0.0, op0=ALU.mult, op1=ALU.add))
                            xr, xi = zi, t2
                            c0, s0 = Er.t[:, :, 1:NB], Ei.t[:, :, 1:NB]
                            xbr = Xbf.t[:, :, 0, :].rearrange("p (g d) n -> p g d n", d=2)[:, 4 * hf:4 * hf + 4, dr, :]
                            xbi = Xbf.t[:, :, 1, :].rearrange("p (g d) n -> p g d n", d=2)[:, 4 * hf:4 * hf + 4, dr, :]
                            V(lambda e: e.memset(xbr[:, :, 0:1], 0.0), w=[Xbf])
                            V(lambda e: e.memset(xbi[:, :, 0:1], 0.0), w=[Xbf])
                            tt(t1[:, :, 0:NB - 1], xr[:, :, 0:NB - 1], c0, ALU.mult)
                            tt(zr[:, :, 0:NB - 1], xi[:, :, 0:NB - 1], s0, ALU.mult)
                            V(lambda e: e.tensor_tensor(out=xbr[:, :, 1:NB], in0=t1[:, :, 0:NB - 1], in1=zr[:, :, 0:NB - 1], op=ALU.subtract), w=[Xbf])
                            tt(t1[:, :, 0:NB - 1], xr[:, :, 0:NB - 1], s0, ALU.mult)
                            tt(zr[:, :, 0:NB - 1], xi[:, :, 0:NB - 1], c0, ALU.mult)
                            V(lambda e: e.tensor_tensor(out=xbi[:, :, 1:NB], in0=t1[:, :, 0:NB - 1], in1=zr[:, :, 0:NB - 1], op=ALU.add), w=[Xbf])
                    P.barrier()
                with ExitStack() as es:
                    yt_p = RR([P.sb(es, [128, 3, 512], BF16, "ytp") for _ in range(2)])
                    for tix in range(NT):
                        t0 = tix * 512
                        nb0 = t0 // 8
                        yt = yt_p.get()
                        for ti in range(3):
                            ps = psf.get()
                            pv = ps.t[:, 0:512].rearrange("p (n i) -> p n i", i=8)
                            uv = uTs.t[:, ti, t0:t0 + 512].rearrange("p (n i) -> p n i", i=8)
                            mms = [(ps.t[:, 0:512], TAPS.t[0:64, ti, 7, :], uTs.t[0:64, ti, t0:t0 + 512])]
                            for tau in range(1, 8):
                                for i in range(tau, 8):
                                    mms.append((pv[:, :, i], TAPS.t[:, ti, 7 + tau, :], uv[:, :, i - tau]))
                                for i in range(0, 8 - tau):
                                    mms.append((pv[:, :, i], TAPS.t[:, ti, 7 - tau, :], uv[:, :, i + tau]))
                            npairs = 3 if ti < 2 else 2
                            for sl in range(npairs):
                                gp = 3 * ti + sl
                                rs = slice(32 * sl, 32 * sl + 32)
                                pvs = ps.t[rs, 0:512].rearrange("p (n i) -> p n i", i=8)
                                for dr in range(2):
                                    un = gp * 2 + dr
                                    for i in range(8):
                                        for ri in range(2):
                                            if dr == 0:
                                                rhs = Xbf.t[:, un, ri, nb0:nb0 + 64]
                                            else:
                                                a0 = NB - 1 - nb0
                                                rhs = Xbf.t[:, un, ri, a0 - 63:a0 + 1][:, ::-1]
                                            mms.append((pvs[:, :, i], WY.t[:, un, i, ri, :], rhs))
                            mms.append((ps.t[:, 0:512], TAPS.t[64:128, ti, 7, :], uTs.t[64:128, ti, t0:t0 + 512]))
                            nm = len(mms)

                            def emit(e):
                                ins = None
                                for ix, (o, lh, rh) in enumerate(mms):
                                    ins = e.matmul(o, lhsT=lh, rhs=rh, start=(ix == 0), stop=(ix == nm - 1))
                                return ins
                            P.op(pe, emit, reads=[TAPS, WY, Xbf, uTs], writes=[ps])
                            P.op(act, lambda e: e.activation(out=yt.t[:, ti, :], in_=ps.t[:, 0:512], func=AF.Gelu), reads=[ps], writes=[yt])
                        P.dma(sp, YTS[:, :, t0:t0 + 512].rearrange("k p t -> p k t"), yt.t[:], reads=[yt])
                    P.barrier()

        def fm_store(key, tile_T, s, t0, n=512):
            P.dma(sp, FMS[key][:, :, t0:t0 + n].rearrange("k p t -> p k t"), tile_T.t[:, :, 0:n], reads=[tile_T])

        WCOLS = {"fm0": (0, 512), "fm1": (512, 512), "fm2": (1024, 512), "fm3": (1536, 416),
                 "tm0": (NFM, 512), "tm1": (NFM + 512, 512), "tm2": (NFM + 1024, 512), "tm3": (NFM + 1536, 512),
                 "tm4": (NFM + 2048, 256)}

        def phase_A(l, s):
            with ExitStack() as es:
                wpool = RR([P.sb(es, [128, 8, 512], BF16, "wA") for _ in range(3)])
                xt = P.sb(es, [128, 8, 512], F32, "xA")
                sq = P.sb(es, [128, 8, 512], BF16, "sqA")
                rstd = P.sb(es, [128, 512], F32, "rstdA")
                h = P.sb(es, [128, 8, 512], BF16, "hA")
                g1 = P.sb(es, [128, 8], F32, "g1")
                P.dma(sp, g1.t[:], pd["g1"][l], writes=[g1])
                tabs = {k: P.sb(es, [128, 512], F32, "tab" + k) for k in ["cosT", "sinT", "cosT8", "sinT8"]}
                tmA = RR([P.sb(es, [128, 512], F32, "tmA") for _ in range(4)])
                qT = P.sb(es, [128, 2, 512], BF16, "qT")
                kT = P.sb(es, [128, 2, 512], BF16, "kT")
                gq = P.sb(es, [128, 2, 512], F32, "gq")
                gk = P.sb(es, [128, 2, 512], F32, "gk")
                qf = P.sb(es, [128, 2, 512], BF16, "qf")
                kf = P.sb(es, [128, 2, 512], BF16, "kf")
                qb = P.sb(es, [128, 2, 512], BF16, "qb")
                kb = P.sb(es, [128, 2, 512], BF16, "kb")
                uT = P.sb(es, [128, 3, 512], BF16, "uTa")
                lrT = P.sb(es, [32, 512], BF16, "lrT")
                P.op(pool, lambda e: e.memset(lrT.t[:], 1.0), writes=[lrT])
                wgt = P.sb(es, [32, 512], BF16, "wgt")
                P.dma(sp, wgt.t[:], wbf["wg"][l], writes=[wgt])
                cm = P.sb(es, [128, 4, 256], F32, "cm")
                sm = P.sb(es, [128, 4, 256], F32, "sm")
                rng = P.sb(es, [128, 256], F32, "rng")
                gng = P.sb(es, [128, 512], F32, "gng")
                P.dma(sp, rng.t[:], pd["rng"][l], writes=[rng])
                P.dma(sp, gng.t[:], pd["gng"][l], writes=[gng])
                krf = P.sb(es, [128, 4, 256], BF16, "krf")
                krb = P.sb(es, [128, 4, 256], BF16, "krb")
                vt = P.sb(es, [128, 4, 256], BF16, "vt")
                rgs = P.sb(es, [128, 4, 256], BF16, "rgs")
                gvt = P.sb(es, [128, 4, 512], BF16, "gvt")
                ggs = P.sb(es, [128, 4, 512], BF16, "ggs")
                gkt = P.sb(es, [128, 4, 256], F32, "gkt")
                lp_t = P.sb(es, [128, 512], F32, "lp")
                kstf = P.sb(es, [128, 256], BF16, "kstf")
                kstb = P.sb(es, [128, 256], BF16, "kstb")
                kvr_t = RR([P.sb(es, [128, 512], F32, "kvr") for _ in range(2)])
                kvg_t = RR([P.sb(es, [128, 1024], F32, "kvg") for _ in range(2)])
                for ti in range(NT):
                    t0 = ti * 512
                    P.dma(sp, xt.t[:], XT[s, :, :, t0:t0 + 512].rearrange("k p t -> p k t"), writes=[xt])
                    for k in tabs:
                        P.dma(sp, tabs[k].t[:], cd[k][:, t0:t0 + 512], writes=[tabs[k]])
                    P.dma(sp, cm.t[:], cd["cosM"][t0:t0 + 512, :].rearrange("(c p) f -> p c f", p=128), writes=[cm])
                    P.dma(sp, sm.t[:], cd["sinM"][t0:t0 + 512, :].rearrange("(c p) f -> p c f", p=128), writes=[sm])
                    rmsnorm(es, xt, g1, h, sq, rstd)
                    P.dma(sp, HS[:, :, t0:t0 + 512].rearrange("k p t -> p k t"), h.t[:], reads=[h])

                    def wload(name):
                        c0, n = WCOLS[name]
                        w = wpool.get()
                        P.dma(sp, w.t[:, :, 0:n], wview("w_in2", l)[:, :, c0:c0 + n], writes=[w])
                        return w

                    def fmm(w, f, m=128):
                        ps = psf.get()
                        P.mm(ps.t[0:m, :], [(w.t[:, k, f * 128:f * 128 + m], h.t[:, k, :]) for k in range(8)], reads=[w, h], writes=[ps])
                        return ps
                    for name, dst, ct, st in (("fm0", qT, "cosT8", "sinT8"), ("fm1", kT, "cosT", "sinT")):
                        w = wload(name)
                        for hp in range(2):
                            pa = fmm(w, 2 * hp)
                            pb = fmm(w, 2 * hp + 1)
                            a = tmA.get()
                            b = tmA.get()
                            P.op(dve, lambda e: e.tensor_tensor(out=a.t[:], in0=pa.t[:], in1=tabs[ct].t[:], op=ALU.mult), reads=[pa, tabs[ct]], writes=[a])
                            P.op(dve, lambda e: e.tensor_tensor(out=b.t[:], in0=pb.t[:], in1=tabs[st].t[:], op=ALU.mult), reads=[pb, tabs[st]], writes=[b])
                            P.op(pool, lambda e: e.tensor_tensor(out=dst.t[:, hp, :], in0=a.t[:], in1=b.t[:], op=ALU.add), reads=[a, b], writes=[dst])
                    fm_store("rq", qT, s, t0)
                    fm_store("rkt", kT, s, t0)
                    w = wload("fm2")
                    for f in range(4):
                        ps = fmm(w, f)
                        dst = gq if f < 2 else gk
                        P.op(act, lambda e: e.activation(out=dst.t[:, f % 2, :], in_=ps.t[:], func=AF.Copy), reads=[ps], writes=[dst])
                    w = wload("fm3")
                    for f in range(3):
                        ps = fmm(w, f)
                        P.op(act, lambda e: e.activation(out=uT.t[:, f, :], in_=ps.t[:], func=AF.Copy), reads=[ps], writes=[uT])
                    P.dma(sp, UTS[:, :, t0:t0 + 512].rearrange("k p t -> p k t"), uT.t[:], reads=[uT])
                    ps = fmm(w, 3, m=16)
                    P.op(act, lambda e: e.activation(out=lrT.t[0:16, :], in_=ps.t[0:16, :], func=AF.Copy), reads=[ps], writes=[lrT])
                    def tmm(w, c, n):
                        ps = psf.get()
                        P.mm(ps.t[:, 0:n], [(h.t[:, k, c * 128:(c + 1) * 128], w.t[:, k, 0:n]) for k in range(8)], reads=[w, h], writes=[ps])
                        return ps
                    w = wload("tm0")
                    for c in range(4):
                        ps = tmm(w, c, 512)
                        a = tmA.get()
                        b = tmA.get()
                        P.op(dve, lambda e: e.tensor_tensor(out=a.t[:, 0:256], in0=ps.t[:, 0:256], in1=cm.t[:, c, :], op=ALU.mult), reads=[ps, cm], writes=[a])
                        P.op(dve, lambda e: e.tensor_tensor(out=b.t[:, 0:256], in0=ps.t[:, 256:512], in1=sm.t[:, c, :], op=ALU.mult), reads=[ps, sm], writes=[b])
                        P.op(pool, lambda e: e.tensor_tensor(out=a.t[:, 0:256], in0=a.t[:, 0:256], in1=b.t[:, 0:256], op=ALU.add), reads=[a, b], writes=[a])
                        P.op(pool, lambda e: e.tensor_tensor(out=krf.t[:, c, :], in0=a.t[:, 0:256], in1=wft.t[:], op=ALU.mult), reads=[a, wft], writes=[krf])
                        P.op(pool, lambda e: e.tensor_tensor(out=krb.t[:, c, :], in0=a.t[:, 0:256], in1=wbt.t[:], op=ALU.mult), reads=[a, wbt], writes=[krb])
                    w = wload("tm1")
                    for c in range(4):
                        ps = tmm(w, c, 512)
                        a = tmA.get()
                        P.op(act, lambda e: e.activation(out=vt.t[:, c, :], in_=ps.t[:, 0:256], func=AF.Copy), reads=[ps], writes=[vt])
                        P.op(act, lambda e: e.activation(out=a.t[:, 0:256], in_=ps.t[:, 256:512], func=AF.Silu), reads=[ps], writes=[a])
                        P.op(pool, lambda e: e.tensor_tensor(out=rgs.t[:, c, :], in0=a.t[:, 0:256], in1=rng.t[:], op=ALU.mult), reads=[a, rng], writes=[rgs])
                    P.dma(sp, RV[t0:t0 + 512, :].rearrange("(c p) f -> p c f", p=128), vt.t[:], reads=[vt])
                    P.dma(sp, RGS[t0:t0 + 512, :].rearrange("(c p) f -> p c f", p=128), rgs.t[:], reads=[rgs])
                    w = wload("tm2")
                    for c in range(4):
                        ps = tmm(w, c, 512)
                        P.op(act, lambda e: e.activation(out=gvt.t[:, c, :], in_=ps.t[:], func=AF.Copy), reads=[ps], writes=[gvt])
                    P.dma(sp, GV[t0:t0 + 512, :].rearrange("(c p) f -> p c f", p=128), gvt.t[:], reads=[gvt])
                    w = wload("tm3")
                    for c in range(4):
                        ps = tmm(w, c, 512)
                        a = tmA.get()
                        P.op(act, lambda e: e.activation(out=a.t[:], in_=ps.t[:], func=AF.Silu), reads=[ps], writes=[a])
                        P.op(pool, lambda e: e.tensor_tensor(out=ggs.t[:, c, :], in0=a.t[:], in1=gng.t[:], op=ALU.mult), reads=[a, gng], writes=[ggs])
                    P.dma(sp, GGS[t0:t0 + 512, :].rearrange("(c p) f -> p c f", p=128), ggs.t[:], reads=[ggs])
                    w = wload("tm4")
                    for c in range(4):
                        ps = tmm(w, c, 256)
                        P.op(act, lambda e: e.activation(out=gkt.t[:, c, :], in_=ps.t[:, 0:256], func=AF.Copy), reads=[ps], writes=[gkt])
                    for c in range(4):
                        ch = ti * 4 + c
                        cs = slice(c * 128, (c + 1) * 128)
                        ps = psf.get()
                        P.mm(ps.t[:, :], [(lrT.t[0:32, cs], wgt.t[0:32, :])], reads=[lrT, wgt], writes=[ps])
                        a = tmA.get()
                        P.op(act, lambda e: e.activation(out=a.t[:], in_=ps.t[:], func=AF.Exp, scale=-1.0), reads=[ps], writes=[a])
                        P.op(act, lambda e: e.activation(out=lp_t.t[:], in_=a.t[:], func=AF.Ln, bias=one_t.t[:, 0:1]), reads=[a, one_t], writes=[lp_t])
                        ps = psf.get()
                        for dr in range(2):
                            for hp in range(2):
                                blk = dr * 2 + hp
                                mk = m_le if dr == 0 else m_ge
                                P.mm(ps.t[:, blk * 128:(blk + 1) * 128], [(lp_t.t[:, dr * 256 + hp * 128: dr * 256 + (hp + 1) * 128], mk.t[:])],
                                     reads=[lp_t, mk], writes=[ps])
                        en = tmA.get()
                        ep = tmA.get()
                        P.op(act, lambda e: e.activation(out=en.t[:], in_=ps.t[:], func=AF.Exp, scale=-1.0 / 16), reads=[ps], writes=[en])
                        P.op(act, lambda e: e.activation(out=ep.t[:], in_=ps.t[:], func=AF.Exp, scale=1.0 / 16), reads=[ps], writes=[ep])
                        for dr in range(2):
                            for hp in range(2):
                                blk = dr * 2 + hp
                                bs = slice(blk * 128, (blk + 1) * 128)
                                qd = qf if dr == 0 else qb
                                kd = kf if dr == 0 else kb
                                P.op(dve, lambda e: e.scalar_tensor_tensor(out=qd.t[:, hp, cs], in0=gq.t[:, hp, cs], scalar=0.125, in1=en.t[:, bs],
                                                                           op0=ALU.mult, op1=ALU.mult), reads=[gq, en], writes=[qd])
                                P.op(pool, lambda e: e.tensor_tensor(out=kd.t[:, hp, cs], in0=gk.t[:, hp, cs], in1=ep.t[:, bs], op=ALU.mult),
                                     reads=[gk, ep], writes=[kd])
                        env = en.t[:].rearrange("p (d h i) -> p d h i", d=2, h=2)
                        P.op(pool, lambda e: e.tensor_copy(out=decg.t[:, ch, 0, :], in_=env[:, 0, :, 127]), reads=[en], writes=[decg])
                        P.op(pool, lambda e: e.tensor_copy(out=decg.t[:, ch, 1, :], in_=env[:, 1, :, 0]), reads=[en], writes=[decg])
                        ps = psf.get()
                        P.mm(ps.t[:, 0:256], [(m_gt.t[:], lp_t.t[:, 0:256])], reads=[lp_t, m_gt], writes=[ps])
                        P.mm(ps.t[:, 256:512], [(m_lt.t[:], lp_t.t[:, 256:512])], reads=[lp_t, m_lt], writes=[ps])
                        a = tmA.get()
                        P.op(act, lambda e: e.activation(out=a.t[:], in_=ps.t[:], func=AF.Exp, scale=-1.0 / 16), reads=[ps], writes=[a])
                        P.op(dve, lambda e: e.tensor_tensor(out=kstf.t[:], in0=gkt.t[:, c, :], in1=a.t[:, 0:256], op=ALU.mult), reads=[gkt, a], writes=[kstf])
                        P.op(pool, lambda e: e.tensor_tensor(out=kstb.t[:], in0=gkt.t[:, c, :], in1=a.t[:, 256:512], op=ALU.mult), reads=[gkt, a], writes=[kstb])
                        ps = psf.get()
                        for dr in range(2):
                            for hp in range(2):
                                blk = dr * 2 + hp
                                kk = krf if dr == 0 else krb
                                P.mm(ps.t[:, blk * 128:(blk + 1) * 128], [(kk.t[:, c, hp * 128:(hp + 1) * 128], vt.t[:, c, hp * 128:(hp + 1) * 128])],
                                     reads=[kk, vt], writes=[ps])
                        kv = kvr_t.get()
                        P.op(act, lambda e: e.activation(out=kv.t[:], in_=ps.t[:], func=AF.Copy), reads=[ps], writes=[kv])
                        P.dma(sp, KVR[ch], kv.t[:], reads=[kv])
                        kvg = kvg_t.get()
                        for dr in range(2):
                            ps = psf.get()
                            kk = kstf if dr == 0 else kstb
                            for hp in range(2):
                                P.mm(ps.t[:, hp * 256:(hp + 1) * 256], [(kk.t[:, hp * 128:(hp + 1) * 128], gvt.t[:, c, hp * 256:(hp + 1) * 256])],
                                     reads=[kk, gvt], writes=[ps])
                            P.op(act if dr == 0 else dve,
                                 (lambda e: e.activation(out=kvg.t[:, 0:512], in_=ps.t[:], func=AF.Copy)) if dr == 0 else
                                 (lambda e: e.tensor_copy(out=kvg.t[:, 512:1024], in_=ps.t[:])), reads=[ps], writes=[kvg])
                        P.dma(sp, KVG[ch], kvg.t[:], reads=[kvg])
                    for key, tl in (("gqf", qf), ("gkf", kf), ("gqb", qb), ("gkb", kb)):
                        fm_store(key, tl, s, t0)
                P.barrier()

        def phase_B(l, s):
            with ExitStack() as es:
                kvr_in = RR([P.sb(es, [128, 512], F32, "kvri") for _ in range(4)])
                kvg_in = RR([P.sb(es, [128, 1024], F32, "kvgi") for _ in range(4)])
                srs = P.sb(es, [128, 2, 256], F32, "srs")
                sgs = P.sb(es, [128, 2, 512], F32, "sgs")
                P.op(pool, lambda e: e.memset(srs.t[:], 0.0), writes=[srs])
                P.op(pool, lambda e: e.memset(sgs.t[:], 0.0), writes=[sgs])
                sro = RR([P.sb(es, [128, 256], BF16, "sro") for _ in range(4)])
                sgo = RR([P.sb(es, [128, 512], BF16, "sgo") for _ in range(4)])
                for step in range(NCH):
                    for dr in range(2):
                        n = step if dr == 0 else NCH - 1 - step
                        kr = kvr_in.get()
                        kg = kvg_in.get()
                        P.dma(sp, kr.t[:, 0:256], KVR[n, :, dr * 256:(dr + 1) * 256], writes=[kr])
                        P.dma(sp, kg.t[:, 0:512], KVG[n, :, dr * 512:(dr + 1) * 512], writes=[kg])
                        o1 = sro.get()
                        o2 = sgo.get()
                        P.op(pool, lambda e: e.tensor_tensor(out=o1.t[:], in0=srs.t[:, dr, :], in1=bd_r.t[:], op=ALU.mult), reads=[srs, bd_r], writes=[o1])
                        P.op(pool, lambda e: e.tensor_tensor(out=o2.t[:], in0=sgs.t[:, dr, :], in1=bd_g.t[:], op=ALU.mult), reads=[sgs, bd_g], writes=[o2])
                        P.dma(sp, SR[n, :, dr, :], o1.t[:], reads=[o1])
                        P.dma(sp, SG[n, :, dr, :], o2.t[:], reads=[o2])
                        for hp in range(2):
                            P.op(dve, lambda e: e.scalar_tensor_tensor(out=srs.t[:, dr, hp * 128:(hp + 1) * 128], in0=srs.t[:, dr, hp * 128:(hp + 1) * 128],
                                                                       scalar=decr.t[:, hp:hp + 1], in1=kr.t[:, hp * 128:(hp + 1) * 128],
                                                                       op0=ALU.mult, op1=ALU.add), reads=[srs, kr, decr], writes=[srs])
                            P.op(dve, lambda e: e.scalar_tensor_tensor(out=sgs.t[:, dr, hp * 256:(hp + 1) * 256], in0=sgs.t[:, dr, hp * 256:(hp + 1) * 256],
                                                                       scalar=decg.t[:, n, dr, hp:hp + 1], in1=kg.t[:, hp * 256:(hp + 1) * 256],
                                                                       op0=ALU.mult, op1=ALU.add), reads=[sgs, kg, decg], writes=[sgs])
                P.barrier()
            phase_B2(l, s)

        def phase_C(l, s):
            with ExitStack() as es:
                wpool = RR([P.sb(es, [128, 8, 512], BF16, "wC") for _ in range(3)])
                xt = P.sb(es, [128, 8, 512], F32, "xC")
                h = P.sb(es, [128, 8, 512], BF16, "hC")
                sq = P.sb(es, [128, 8, 512], BF16, "sqC")
                rstd = P.sb(es, [128, 512], F32, "rstdC")
                g2 = P.sb(es, [128, 8], F32, "g2")
                bm = P.sb(es, [128, 24], F32, "bm")
                P.dma(sp, g2.t[:], pd["g2"][l], writes=[g2])
                P.dma(sp, bm.t[:], pd["bm"][l], writes=[bm])
                yT = P.sb(es, [128, 3, 512], BF16, "yTc")
                qT = P.sb(es, [128, 2, 512], BF16, "qTc")
                kT = P.sb(es, [128, 2, 512], BF16, "kTc")
                fmt = {k: P.sb(es, [128, 2, 512], BF16, "c" + k) for k in ["gqf", "gkf", "gqb", "gkb"]}
                vt = P.sb(es, [128, 4, 256], BF16, "vtc")
                rgs = P.sb(es, [128, 4, 256], BF16, "rgsc")
                gvt = P.sb(es, [128, 4, 512], BF16, "gvtc")
                ggs = P.sb(es, [128, 4, 512], BF16, "ggsc")
                srt = RR([P.sb(es, [128, 2, 256], BF16, "srt") for _ in range(2)])
                sgt = RR([P.sb(es, [128, 2, 512], BF16, "sgt") for _ in range(2)])
                qfr = P.sb(es, [128, 2, 128], BF16, "qfr")
                qbr = P.sb(es, [128, 2, 128], BF16, "qbr")
                sd = RR([P.sb(es, [128, 512], BF16, "sd") for _ in range(3)])
                tmC = RR([P.sb(es, [128, 512], F32, "tmC") for _ in range(4)])
                st4 = RR([P.sb(es, [128, 16], F32, "st4") for _ in range(4)])
                rob = P.sb(es, [128, 256], BF16, "rob")
                gob = P.sb(es, [128, 512], BF16, "gob")
                roT = P.sb(es, [128, 2, 512], BF16, "roT")
                goT = P.sb(es, [128, 4, 512], BF16, "goT")
                brs = P.sb(es, [128, 8, 512], F32, "brs")
                mrg = P.sb(es, [128, 8, 512], BF16, "mrg")
                mid = P.sb(es, [128, 32, 512], BF16, "mid")
                gl = RR([P.sb(es, [128, 512], F32, "gl") for _ in range(3)])

                def wload(key, c0, n, kch=8, r0=0):
                    w = wpool.get()
                    P.dma(sp, w.t[:, 0:kch, 0:n], wview(key, l)[:, r0:r0 + kch, c0:c0 + n], writes=[w])
                    return w

                HN = int(os.environ.get("HN", "99"))

                def headnorm(ps, nh, hd, center, gate_ap, gate_T, outb):
                    W = nh * hd
                    sqt = tmC.get()
                    st = st4.get()
                    nrm = tmC.get()
                    steps = []
                    pcp = tmC.get()
                    P.op(act, lambda e: e.activation(out=pcp.t[:, 0:W], in_=ps.t[:, 0:W], func=AF.Copy), reads=[ps], writes=[pcp])
                    ps = pcp
                    steps.append(lambda: P.op(act, lambda e: e.activation(out=sqt.t[:, 0:W], in_=ps.t[:, 0:W], func=AF.Square), reads=[ps], writes=[sqt]))
                    steps.append(lambda: P.op(dve, lambda e: e.tensor_reduce(out=st.t[:, 0:nh], in_=ps.t[:, 0:W].rearrange("p (h d) -> p h d", h=nh), axis=AX.X, op=ALU.add),
                         reads=[ps], writes=[st]))
                    steps.append(lambda: P.op(dve, lambda e: e.tensor_reduce(out=st.t[:, 4:4 + nh], in_=sqt.t[:, 0:W].rearrange("p (h d) -> p h d", h=nh), axis=AX.X, op=ALU.add),
                         reads=[sqt], writes=[st]))
                    steps.append(lambda: P.op(dve, lambda e: e.tensor_scalar(out=st.t[:, 8:8 + nh], in0=st.t[:, 0:nh], scalar1=(1.0 / hd) if center else 0.0, scalar2=None, op0=ALU.mult),
                         reads=[st], writes=[st]))
                    steps.append(lambda: P.op(dve, lambda e: e.tensor_tensor(out=st.t[:, 12:12 + nh], in0=st.t[:, 8:8 + nh], in1=st.t[:, 8:8 + nh], op=ALU.mult), reads=[st], writes=[st]))
                    steps.append(lambda: P.op(dve, lambda e: e.scalar_tensor_tensor(out=st.t[:, 4:4 + nh], in0=st.t[:, 4:4 + nh], scalar=1.0 / hd, in1=st.t[:, 12:12 + nh],
                                                               op0=ALU.mult, op1=ALU.subtract), reads=[st], writes=[st]))
                    steps.append(lambda: P.op(act, lambda e: e.activation(out=st.t[:, 4:4 + nh], in_=st.t[:, 4:4 + nh], func=AF.Sqrt, bias=eps_t.t[:, 0:1]), reads=[st, eps_t], writes=[st]))
                    steps.append(lambda: P.op(dve, lambda e: e.reciprocal(out=st.t[:, 4:4 + nh], in_=st.t[:, 4:4 + nh]), reads=[st], writes=[st]))

                    def nrm_step():
                        for hh in range(nh):
                            P.op(dve, lambda e: e.tensor_scalar(out=nrm.t[:, hh * hd:(hh + 1) * hd], in0=ps.t[:, hh * hd:(hh + 1) * hd],
                                                                scalar1=st.t[:, 8 + hh:9 + hh], scalar2=st.t[:, 4 + hh:5 + hh], op0=ALU.subtract, op1=ALU.mult),
                                 reads=[ps, st], writes=[nrm])
                    steps.append(nrm_step)
                    steps.append(lambda: P.op(pool, lambda e: e.tensor_tensor(out=outb.t[:, 0:W], in0=nrm.t[:, 0:W], in1=gate_ap, op=ALU.mult), reads=[nrm, gate_T], writes=[outb]))
                    for i, f in enumerate(steps):
                        if i < HN:
                            f()

                for ti in range(NT):
                    t0 = ti * 512
                    P.dma(sp, xt.t[:], XT[s, :, :, t0:t0 + 512].rearrange("k p t -> p k t"), writes=[xt])
                    P.dma(sp, h.t[:], HS[:, :, t0:t0 + 512].rearrange("k p t -> p k t"), writes=[h])
                    P.dma(sp, yT.t[:], YTS[:, :, t0:t0 + 512].rearrange("k p t -> p k t"), writes=[yT])
                    P.dma(sp, qT.t[:], FMS["rq"][:, :, t0:t0 + 512].rearrange("k p t -> p k t"), writes=[qT])
                    P.dma(sp, kT.t[:], FMS["rkt"][:, :, t0:t0 + 512].rearrange("k p t -> p k t"), writes=[kT])
                    for k in fmt:
                        P.dma(sp, fmt[k].t[:], FMS[k][:, :, t0:t0 + 512].rearrange("k p t -> p k t"), writes=[fmt[k]])
                    for tl, src in ((vt, RV), (rgs, RGS), (gvt, GV), (ggs, GGS)):
                        P.dma(sp, tl.t[:], src[t0:t0 + 512, :].rearrange("(c p) f -> p c f", p=128), writes=[tl])
                    for c in range(4):
                        ch = ti * 4 + c
                        cs = slice(c * 128, (c + 1) * 128)
                        sr = srt.get()
                        sg = sgt.get()
                        P.dma(sp, sr.t[:], SR[ch], writes=[sr])
                        P.dma(sp, sg.t[:], SG[ch], writes=[sg])
                        if CST < 2:
                            continue
                        ps = psf.get()
                        for hd_ in range(4):
                            hp, hh = hd_ // 2, hd_ % 2
                            rs = slice(hh * 64, (hh + 1) * 64)
                            P.mm(ps.t[:, hd_ * 128:(hd_ + 1) * 128], [(kT.t[rs, hp, cs], qT.t[rs, hp, cs])], reads=[kT, qT], writes=[ps])
                        sdt = sd.get()
                        P.op(dve, lambda e: e.tensor_tensor(out=sdt.t[:], in0=ps.t[:], in1=dmat.t[:], op=ALU.mult), reads=[ps, dmat], writes=[sdt])
                        P.op(pool, lambda e: e.tensor_tensor(out=qfr.t[:], in0=qT.t[:, :, cs], in1=qft.t[:], op=ALU.mult), reads=[qT, qft], writes=[qfr])
                        P.op(pool, lambda e: e.tensor_tensor(out=qbr.t[:], in0=qT.t[:, :, cs], in1=qbt.t[:], op=ALU.mult), reads=[qT, qbt], writes=[qbr])
                        po = psf.get()
                        for hp in range(2):
                            P.mm(po.t[:, hp * 128:(hp + 1) * 128],
                                 [(qfr.t[:, hp, :], sr.t[:, 0, hp * 128:(hp + 1) * 128]), (qbr.t[:, hp, :], sr.t[:, 1, hp * 128:(hp + 1) * 128])],
                                 reads=[qfr, qbr, sr], writes=[po], stop=False)
                            for hh in range(2):
                                hd_ = 2 * hp + hh
                                P.mm(po.t[:, hd_ * 64:(hd_ + 1) * 64], [(sdt.t[:, hd_ * 128:(hd_ + 1) * 128], vt.t[:, c, hd_ * 64:(hd_ + 1) * 64])],
                                     reads=[sdt, vt], writes=[po], start=False, stop=(hh == 1))
                        if CST == 2 and os.environ.get('NOHN'):
                            continue
                        headnorm(po, 4, 64, True, rgs.t[:, c, :], rgs, rob)
                        if CST == 2 and os.environ.get('NOTR'):
                            continue
                        pt = psb.get()
                        for k in range(2):
                            P.op(pe, lambda e: e.transpose(out=pt.t[:, k * 128:(k + 1) * 128], in_=rob.t[:, k * 128:(k + 1) * 128], identity=ident_b.t[:]),
                                 reads=[rob, ident_b], writes=[pt])
                        P.op(act, lambda e: e.activation(out=roT.t[:, :, cs], in_=pt.t[:, 0:256].rearrange("p (k t) -> p k t", k=2), func=AF.Copy), reads=[pt], writes=[roT])
                        if CST < 3:
                            continue
                        sdf = sd.get()
                        sdb = sd.get()
                        for dr, (kk, qq, mk, dst) in enumerate(((fmt["gkf"], fmt["gqf"], m_le4, sdf), (fmt["gkb"], fmt["gqb"], m_gt4, sdb))):
                            ps = psf.get()
                            for hd_ in range(4):
                                hp, hh = hd_ // 2, hd_ % 2
                                rs = slice(hh * 64, (hh + 1) * 64)
                                P.mm(ps.t[:, hd_ * 128:(hd_ + 1) * 128], [(kk.t[rs, hp, cs], qq.t[rs, hp, cs])], reads=[kk, qq], writes=[ps])
                            P.op(dve, lambda e: e.tensor_tensor(out=dst.t[:], in0=ps.t[:], in1=mk.t[:], op=ALU.mult), reads=[ps, mk], writes=[dst])
                        po = psf.get()
                        for hp in range(2):
                            P.mm(po.t[:, hp * 256:(hp + 1) * 256],
                                 [(fmt["gqf"].t[:, hp, cs], sg.t[:, 0, hp * 256:(hp + 1) * 256]), (fmt["gqb"].t[:, hp, cs], sg.t[:, 1, hp * 256:(hp + 1) * 256])],
                                 reads=[fmt["gqf"], fmt["gqb"], sg], writes=[po], stop=False)
                            for hh in range(2):
                                hd_ = 2 * hp + hh
                                P.mm(po.t[:, hd_ * 128:(hd_ + 1) * 128],
                                     [(sdf.t[:, hd_ * 128:(hd_ + 1) * 128], gvt.t[:, c, hd_ * 128:(hd_ + 1) * 128]),
                                      (sdb.t[:, hd_ * 128:(hd_ + 1) * 128], gvt.t[:, c, hd_ * 128:(hd_ + 1) * 128])],
                                     reads=[sdf, sdb, gvt], writes=[po], start=False, stop=(hh == 1))
                        headnorm(po, 4, 128, False, ggs.t[:, c, :], ggs, gob)
                        pt = psb.get()
                        for k in range(4):
                            P.op(pe, lambda e: e.transpose(out=pt.t[:, k * 128:(k + 1) * 128], in_=gob.t[:, k * 128:(k + 1) * 128], identity=ident_b.t[:]),
                                 reads=[gob, ident_b], writes=[pt])
                        P.op(act, lambda e: e.activation(out=goT.t[:, :, cs], in_=pt.t[:, 0:512].rearrange("p (k t) -> p k t", k=4), func=AF.Copy), reads=[pt], writes=[goT])
                    if CST < 4:
                        continue
                    def gate_ps(wm_t, f):
                        ps = psf.get()
                        P.mm(ps.t[:, :], [(wm_t.t[:, k, f * 128:(f + 1) * 128], h.t[:, k, :]) for k in range(8)], reads=[wm_t, h], writes=[ps])
                        return ps
                    for fb in range(2):
                        wa = wload("wa", fb * 512, 512, kch=2)
                        wc = wload("wc", fb * 512, 512, kch=4)
                        wga = wload("wm", fb * 512, 512)
                        for f in range(4):
                            fo = fb * 4 + f
                            g = gl.get()
                            ps = gate_ps(wga, f)
                            P.op(act, lambda e: e.activation(out=g.t[:], in_=ps.t[:], func=AF.Sigmoid, bias=bm.t[:, fo:fo + 1]), reads=[ps, bm], writes=[g])
                            pa = psf.get()
                            P.mm(pa.t[:, :], [(wa.t[:, k, f * 128:(f + 1) * 128], roT.t[:, k, :]) for k in range(2)], reads=[wa, roT], writes=[pa])
                            P.op(dve, lambda e: e.tensor_tensor(out=brs.t[:, fo, :], in0=pa.t[:], in1=g.t[:], op=ALU.mult), reads=[pa, g], writes=[brs])
                        wgc = wload("wm", 2048 + fb * 512, 512)
                        for f in range(4):
                            fo = fb * 4 + f
                            g = gl.get()
                            ps = gate_ps(wgc, f)
                            P.op(act, lambda e: e.activation(out=g.t[:], in_=ps.t[:], func=AF.Sigmoid, bias=bm.t[:, 16 + fo:17 + fo]), reads=[ps, bm], writes=[g])
                            pc = psf.get()
                            P.mm(pc.t[:, :], [(wc.t[:, k, f * 128:(f + 1) * 128], goT.t[:, k, :]) for k in range(4)], reads=[wc, goT], writes=[pc])
                            tmp = tmC.get()
                            P.op(dve, lambda e: e.tensor_tensor(out=tmp.t[:], in0=pc.t[:], in1=g.t[:], op=ALU.mult), reads=[pc, g], writes=[tmp])
                            P.op(pool, lambda e: e.tensor_tensor(out=brs.t[:, fo, :], in0=brs.t[:, fo, :], in1=tmp.t[:], op=ALU.add), reads=[brs, tmp], writes=[brs])
                        wgb = wload("wm", 1024 + fb * 512, 512)
                        wb1 = wload("wb2", fb * 512, 512, kch=3)
                        wb2_ = wload("wb2", 1024 + fb * 512, 512, kch=3)
                        for f in range(4):
                            fo = fb * 4 + f
                            g = gl.get()
                            ps = gate_ps(wgb, f)
                            P.op(act, lambda e: e.activation(out=g.t[:], in_=ps.t[:], func=AF.Sigmoid, bias=bm.t[:, 8 + fo:9 + fo]), reads=[ps, bm], writes=[g])
                            p1 = psf.get()
                            P.mm(p1.t[:, :], [(wb1.t[:, k, f * 128:(f + 1) * 128], yT.t[:, k, :]) for k in range(3)], reads=[wb1, yT], writes=[p1])
                            p2 = psf.get()
                            P.mm(p2.t[:, :], [(wb2_.t[:, k, f * 128:(f + 1) * 128], yT.t[:, k, :]) for k in range(3)], reads=[wb2_, yT], writes=[p2])
                            sgm = tmC.get()
                            P.op(act, lambda e: e.activation(out=sgm.t[:], in_=p2.t[:], func=AF.Sigmoid), reads=[p2], writes=[sgm])
                            P.op(pool, lambda e: e.tensor_tensor(out=sgm.t[:], in0=sgm.t[:], in1=g.t[:], op=ALU.mult), reads=[sgm, g], writes=[sgm])
                            tmp = tmC.get()
                            P.op(dve, lambda e: e.tensor_tensor(out=tmp.t[:], in0=p1.t[:], in1=sgm.t[:], op=ALU.mult), reads=[p1, sgm], writes=[tmp])
                            P.op(pool, lambda e: e.tensor_tensor(out=mrg.t[:, fo, :], in0=brs.t[:, fo, :], in1=tmp.t[:], op=ALU.add), reads=[brs, tmp], writes=[mrg])
                    if CST < 5:
                        continue
                    for fb in range(2):
                        wo = wload("wo", fb * 512, 512)
                        for f in range(4):
                            fo = fb * 4 + f
                            ps = psf.get()
                            P.mm(ps.t[:, :], [(wo.t[:, k, f * 128:(f + 1) * 128], mrg.t[:, k, :]) for k in range(8)], reads=[wo, mrg], writes=[ps])
                            P.op(dve, lambda e: e.tensor_tensor(out=xt.t[:, fo, :], in0=xt.t[:, fo, :], in1=ps.t[:], op=ALU.add), reads=[xt, ps], writes=[xt])
                    if CST < 6:
                        continue
                    rmsnorm(es, xt, g2, h, sq, rstd)
                    for fb in range(8):
                        w1 = wload("wf1", fb * 512, 512)
                        for f in range(4):
                            fo = fb * 4 + f
                            ps = psf.get()
                            P.mm(ps.t[:, :], [(w1.t[:, k, f * 128:(f + 1) * 128], h.t[:, k, :]) for k in range(8)], reads=[w1, h], writes=[ps])
                            r = tmC.get()
                            P.op(act, lambda e: e.activation(out=r.t[:], in_=ps.t[:], func=AF.Relu), reads=[ps], writes=[r])
                            P.op(pool, lambda e: e.tensor_tensor(out=mid.t[:, fo, :], in0=r.t[:], in1=r.t[:], op=ALU.mult), reads=[r], writes=[mid])
                    for fb in range(2):
                        pss = [psf.get() for _ in range(4)]
                        for kb in range(4):
                            w2 = wload("wf2", fb * 512, 512, kch=8, r0=kb * 8)
                            for f in range(4):
                                P.mm(pss[f].t[:, :], [(w2.t[:, k, f * 128:(f + 1) * 128], mid.t[:, kb * 8 + k, :]) for k in range(8)],
                                     reads=[w2, mid], writes=[pss[f]], start=(kb == 0), stop=(kb == 3))
                        for f in range(4):
                            fo = fb * 4 + f
                            P.op(dve, lambda e: e.tensor_tensor(out=xt.t[:, fo, :], in0=xt.t[:, fo, :], in1=pss[f].t[:], op=ALU.add), reads=[xt, pss[f]], writes=[xt])
                    P.dma(sp, XT[s, :, :, t0:t0 + 512].rearrange("k p t -> p k t"), xt.t[:], reads=[xt])
                P.barrier()

        for l in range(DEPTH):
            for s in range(NSEQ):
                import os
                ph = os.environ.get("PH", "ABC")
                if "A" in ph:
                    phase_A(l, s)
                if "B" in ph:
                    phase_B(l, s)
                if "C" in ph:
                    phase_C(l, s)

        with ExitStack() as es:
            xp = RR([P.sb(es, [128, 8, 512], F32, "fx") for _ in range(2)])
            yo = RR([P.sb(es, [128, 8, 512], F32, "fy") for _ in range(2)])
            sq = P.sb(es, [128, 8, 512], BF16, "fsq")
            rstd = P.sb(es, [128, 512], F32, "frstd")
            ot = RR([P.sb(es, [128, D], F32, "fo") for _ in range(2)])
            for s in range(NSEQ):
                for ti in range(NT):
                    xt = xp.get()
                    P.dma(sp, xt.t[:], XT[s, :, :, ti * 512:(ti + 1) * 512].rearrange("k p t -> p k t"), writes=[xt])
                    y = yo.get()
                    rmsnorm(es, xt, gfin, y, sq, rstd)
                    for c in range(4):
                        o_t = ot.get()
                        for half in range(2):
                            ps = psf.get()
                            for kk in range(4):
                                k = half * 4 + kk
                                P.op(pe, lambda e, k=k, kk=kk: e.transpose(out=ps.t[:, kk * 128:(kk + 1) * 128],
                                                                             in_=y.t[:, k, c * 128:(c + 1) * 128], identity=ident_f.t[:]),
                                     reads=[y, ident_f], writes=[ps])
                            if half == 0:
                                P.op(act, lambda e: e.activation(out=o_t.t[:, 0:512], in_=ps.t[:], func=AF.Copy), reads=[ps], writes=[o_t])
                            else:
                                P.op(dve, lambda e: e.tensor_copy(out=o_t.t[:, 512:1024], in_=ps.t[:]), reads=[ps], writes=[o_t])
                        r0 = ti * 512 + c * 128
                        P.dma(sp, out_d[s, r0:r0 + 128, :], o_t.t[:], reads=[o_t])
            P.barrier()
        print('INSTR', {e.name: e.count for e in P.engs}, 'dma', sum(P.dma_val) // 16, flush=True)
    return nc


def kernel(**inputs):
    inp = {k: np.asarray(v) for k, v in inputs.items()}
    B, L, _ = inp["x"].shape
    nseq = B // NCORES
    consts = host_consts(L)
    lp = host_layer_params(inp, DEPTH_FULL)
    nc = build(L, DEPTH_FULL, nseq, consts, lp)
    in_maps = []
    for c in range(NCORES):
        m = {"x": np.ascontiguousarray(inp["x"][c * nseq:(c + 1) * nseq])}
        m.update({"c_" + k: v for k, v in consts.items()})
        m.update({"p_" + k: v for k, v in lp.items()})
        in_maps.append(m)
    res = run_bass_kernel_spmd(nc, in_maps, core_ids=list(range(NCORES)))
    return np.concatenate([r["out"] for r in res.results], axis=0).astype(np.float32)
```

```python
import math
import numpy as np
import concourse.bass as bass
import concourse.mybir as mybir
from concourse.bass_utils import run_bass_kernel_spmd
from contextlib import ExitStack

F32 = mybir.dt.float32
BF16 = mybir.dt.bfloat16
I32 = mybir.dt.int32
AF = mybir.ActivationFunctionType
ALU = mybir.AluOpType
AX = mybir.AxisListType

D = 1024
DEPTH_FULL = 4
L_FULL = 4096
NCORES = 8
EPS = 1e-6
T1 = 8
NOCAST = False
import os
CST = int(os.environ.get('CST', '9'))
NFM = 1952
NTM = 2304
NIN2 = NFM + NTM


class Buf:
    __slots__ = ("w", "r")

    def __init__(self):
        self.w = None
        self.r = {}


class T:
    def __init__(self, t):
        self.t = t
        self.b = Buf()


class RR:
    def __init__(self, tiles):
        self.tiles = tiles
        self.i = 0

    def get(self):
        t = self.tiles[self.i % len(self.tiles)]
        self.i += 1
        return t


class Eng:
    def __init__(self, nc, obj, name, es):
        self.obj = obj
        self.name = name
        self.sem = es.enter_context(nc.semaphore("sem_" + name))
        self.count = 0
        self.seen = {}


class Prog:
    def __init__(self, nc, es, ndma=40):
        self.nc = nc
        self.pe = Eng(nc, nc.tensor, "pe", es)
        self.dve = Eng(nc, nc.vector, "dve", es)
        self.act = Eng(nc, nc.scalar, "act", es)
        self.pool = Eng(nc, nc.gpsimd, "pool", es)
        self.sp = Eng(nc, nc.sync, "sp", es)
        self.engs = [self.pe, self.dve, self.act, self.pool, self.sp]
        self.dma_sems = [es.enter_context(nc.semaphore("dsem%d" % i)) for i in range(ndma)]
        self.dma_val = [0] * ndma
        self.dma_next = 0
        self.uid = 0

    def name(self, p):
        self.uid += 1
        return "%s_%d" % (p, self.uid)

    def sb(self, es, shape, dt, nm="t"):
        return T(es.enter_context(self.nc.sbuf_tensor(self.name(nm), list(shape), dt)))

    def ps(self, es, shape, dt, nm="ps"):
        return T(es.enter_context(self.nc.psum_tensor(self.name(nm), list(shape), dt)))

    def _waits(self, eng, reads, writes):
        need = {}
        for b in reads:
            if b.w is not None:
                s, v = b.w
                if need.get(s, 0) < v:
                    need[s] = v
        for b in writes:
            if b.w is not None:
                s, v = b.w
                if need.get(s, 0) < v:
                    need[s] = v
            for s, v in b.r.items():
                if need.get(s, 0) < v:
                    need[s] = v
        for s, v in need.items():
            if eng.seen.get(s, 0) < v:
                eng.obj.wait_ge(s, v)
                eng.seen[s] = v

    def _post(self, s, v, reads, writes):
        for b in reads:
            if b.r.get(s, 0) < v:
                b.r[s] = v
        for b in writes:
            b.w = (s, v)
            b.r = {}

    def op(self, eng, emit, reads=(), writes=()):
        reads = [x.b if isinstance(x, T) else x for x in reads]
        writes = [x.b if isinstance(x, T) else x for x in writes]
        self._waits(eng, reads, writes)
        ins = emit(eng.obj)
        eng.count += 1
        ins.then_inc(eng.sem, 1)
        self._post(eng.sem, eng.count, reads, writes)

    def mm(self, out, pairs, reads=(), writes=(), start=True, stop=True):
        n = len(pairs)

        def emit(e):
            ins = None
            for i, (l, r) in enumerate(pairs):
                ins = e.matmul(out, lhsT=l, rhs=r, start=(start and i == 0), stop=(stop and i == n - 1))
            return ins
        self.op(self.pe, emit, reads, writes)

    def dma(self, eng, out, in_, reads=(), writes=()):
        reads = [x.b if isinstance(x, T) else x for x in reads]
        writes = [x.b if isinstance(x, T) else x for x in writes]
        if not writes and eng is self.sp:
            eng = self.act
        self._waits(eng, reads, writes)
        i = self.dma_next
        self.dma_next = (i + 1) % len(self.dma_sems)
        s = self.dma_sems[i]
        if self.dma_val[i] > 0 and eng.seen.get(s, 0) < self.dma_val[i]:
            eng.obj.wait_ge(s, self.dma_val[i])
            eng.seen[s] = self.dma_val[i]
        self.dma_val[i] += 16
        eng.obj.dma_start(out=out, in_=in_).then_inc(s, 16)
        self._post(s, self.dma_val[i], reads, writes)

    def barrier(self):
        for e in self.engs:
            for f in self.engs:
                if f is not e and f.count > 0 and e.seen.get(f.sem, 0) < f.count:
                    e.obj.wait_ge(f.sem, f.count)
                    e.seen[f.sem] = f.count
            for i, s in enumerate(self.dma_sems):
                if self.dma_val[i] > 0 and e.seen.get(s, 0) < self.dma_val[i]:
                    e.obj.wait_ge(s, self.dma_val[i])
                    e.seen[s] = self.dma_val[i]


def u_part(g, c):
    gp = g // 2
    return gp // 3, 32 * (gp % 3) + 16 * (g % 2) + c


def host_consts(L):
    f32 = np.float32
    c = {}
    pos = np.arange(L, dtype=f32)
    inv = (1.0 / (10000.0 ** (np.arange(0, 64, 2, dtype=f32) / 64.0))).astype(f32)
    ang = (pos[:, None] * inv[None, :]).astype(f32)
    cos = np.cos(ang).astype(f32)
    sin = np.sin(ang).astype(f32)
    cos64 = np.concatenate([cos, cos], 1)
    sin64 = np.concatenate([-sin, sin], 1)
    c["cosT"] = np.ascontiguousarray(np.tile(cos64.T, (2, 1)))
    c["sinT"] = np.ascontiguousarray(np.tile(sin64.T, (2, 1)))
    c["cosT8"] = (c["cosT"] * f32(0.125)).astype(f32)
    c["sinT8"] = (c["sinT"] * f32(0.125)).astype(f32)
    c["cosM"] = np.ascontiguousarray(np.tile(cos64, (1, 4)))
    c["sinM"] = np.ascontiguousarray(np.tile(sin64, (1, 4)))
    h = np.arange(4, dtype=np.float64)
    log_g = np.log1p(-np.exp2(-5.0 - h))
    idx = np.arange(128, dtype=np.float64)
    dmat = np.exp(log_g[:, None, None] * np.abs(idx[:, None] - idx[None, :]))
    c["dmat"] = np.ascontiguousarray(dmat.transpose(1, 0, 2).reshape(128, 512)).astype(f32)
    w_f = np.exp(log_g[None, :] * (127.0 - idx)[:, None])
    w_b = np.exp(log_g[None, :] * idx[:, None])
    c["wft"] = np.repeat(w_f, 64, axis=1).astype(f32)
    c["wbt"] = np.repeat(w_b, 64, axis=1).astype(f32)
    q_f = np.exp(log_g[None, :] * (idx + 1.0)[:, None])
    q_b = np.exp(log_g[None, :] * (128.0 - idx)[:, None])
    qft = np.zeros((128, 2, 128))
    qbt = np.zeros((128, 2, 128))
    decr = np.zeros((128, 2))
    for hp in range(2):
        for hh in range(2):
            hd = 2 * hp + hh
            qft[hh * 64:(hh + 1) * 64, hp, :] = q_f[:, hd][None, :]
            qbt[hh * 64:(hh + 1) * 64, hp, :] = q_b[:, hd][None, :]
            decr[hh * 64:(hh + 1) * 64, hp] = np.exp(log_g[hd] * 128.0)
    c["qft"] = qft.astype(f32)
    c["qbt"] = qbt.astype(f32)
    c["decr"] = decr.astype(f32)
    j = np.arange(128)[:, None]
    i = np.arange(128)[None, :]
    c["m_le"] = (j <= i).astype(f32)
    c["m_ge"] = (j >= i).astype(f32)
    c["m_gt"] = (j > i).astype(f32)
    c["m_lt"] = (j < i).astype(f32)
    c["m_le4"] = np.tile(c["m_le"], (1, 4))
    c["m_gt4"] = np.tile(c["m_gt"], (1, 4))
    c["ident"] = np.eye(128, dtype=f32)
    bd = np.zeros((128, 128), f32)
    bd[0:64, 0:64] = 1
    bd[64:128, 64:128] = 1
    c["bd_r"] = np.tile(bd, (1, 2))
    bdg = np.zeros((128, 256), f32)
    bdg[0:64, 0:128] = 1
    bdg[64:128, 128:256] = 1
    c["bd_g"] = np.tile(bdg, (1, 2))
    mb = np.zeros((128, 128), f32)
    for p in range(96):
        gg = (p % 32) // 16
        mb[p, gg * 64:(gg + 1) * 64] = 1
    c["maskb"] = mb
    return c


def host_layer_params(inp, depth):
    f32 = np.float32
    o = {}
    w_in = inp["w_in"][:depth]
    rq, rk, rv, rg = w_in[:, :, 0:256], w_in[:, :, 256:512], w_in[:, :, 512:768], w_in[:, :, 768:1024]
    u = w_in[:, :, 1024:1280]
    gq, gk, gv = w_in[:, :, 1280:1536], w_in[:, :, 1536:1792], w_in[:, :, 1792:2304]
    lr, gg = w_in[:, :, 2304:2320], w_in[:, :, 2320:2832]
    perm = np.concatenate([(np.arange(64) + 32) % 64 + 64 * hd for hd in range(4)])
    rqp, rkp = rq[:, :, perm], rk[:, :, perm]
    up = np.zeros((depth, D, 384), f32)
    for g in range(16):
        for cc in range(16):
            ti, p = u_part(g, cc)
            up[:, :, ti * 128 + p] = u[:, :, g * 16 + cc]
    lrp = np.zeros((depth, D, 32), f32)
    lrp[:, :, :16] = lr
    fm = [rq[:, :, 0:128], rqp[:, :, 0:128], rq[:, :, 128:256], rqp[:, :, 128:256],
          rk[:, :, 0:128], rkp[:, :, 0:128], rk[:, :, 128:256], rkp[:, :, 128:256],
          gq, gk, up, lrp]
    tm = [rk, rkp, rv, rg, gv, gg, gk]
    o["w_in2"] = np.ascontiguousarray(np.concatenate(fm + tm, axis=2))
    assert o["w_in2"].shape[2] == NIN2
    wb = inp["w_branch_b"][:depth]
    wb2 = np.zeros((depth, 384, 2 * D), f32)
    for g in range(16):
        for cc in range(16):
            ti, p = u_part(g, cc)
            wb2[:, ti * 128 + p, :] = wb[:, g * 16 + cc, :]
    o["wb2"] = wb2
    o["wa"] = np.ascontiguousarray(inp["w_branch_a"][:depth])
    o["wc"] = np.ascontiguousarray(inp["w_branch_c"][:depth])
    o["wm"] = np.ascontiguousarray(inp["w_merge_gate"][:depth])
    o["wo"] = np.ascontiguousarray(inp["w_out"][:depth])
    o["wf1"] = np.ascontiguousarray(inp["w_ff1"][:depth])
    o["wf2"] = np.ascontiguousarray(inp["w_ff2"][:depth])
    wg = np.zeros((depth, 32, 512), f32)
    wg[:, 0:16, 0:256] = inp["gla_w_gate"][:depth, 0]
    wg[:, 0:16, 256:512] = inp["gla_w_gate"][:depth, 1]
    wg[:, 16, 0:256] = inp["gla_b_gate"][:depth, 0]
    wg[:, 16, 256:512] = inp["gla_b_gate"][:depth, 1]
    o["wg"] = wg

    def pk(v, k):
        return np.ascontiguousarray(v.reshape(v.shape[0], k, 128).transpose(0, 2, 1))
    o["g1"] = pk(inp["norm1_g"][:depth], 8)
    o["g2"] = pk(inp["norm2_g"][:depth], 8)
    o["gfin"] = pk(inp["final_norm_g"][None, :], 8)[0]
    o["bm"] = pk(inp["b_merge_gate"][:depth], 24)
    o["rng"] = np.ascontiguousarray(np.broadcast_to(inp["ret_norm_g"][:depth, None, :], (depth, 128, 256)))
    o["gng"] = np.ascontiguousarray(np.broadcast_to(inp["gla_norm_g"][:depth, None, :], (depth, 128, 512)))
    lre, lim, ldt = inp["s5_lam_re"][:depth], inp["s5_lam_im"][:depth], inp["s5_log_dt"][:depth]
    bre, bim = inp["s5_b_re"][:depth], inp["s5_b_im"][:depth]
    cre, cim = inp["s5_c_re"][:depth], inp["s5_c_im"][:depth]
    lamc = np.zeros((depth, 128, 3, 16), f32)
    bc = np.zeros((depth, 128, 2, 16, 16), f32)
    ccm = np.zeros((depth, 128, 2, 16, 16), f32)
    for gp in range(8):
        for dr in range(2):
            un = gp * 2 + dr
            for g2 in range(2):
                g = 2 * gp + g2
                sl = slice(g2 * 64, (g2 + 1) * 64)
                lamc[:, sl, 0, un] = lre[:, dr, g, :]
                lamc[:, sl, 1, un] = lim[:, dr, g, :]
                lamc[:, sl, 2, un] = ldt[:, dr, g][:, None]
                bc[:, sl, 0, un, :] = bre[:, dr, g]
                bc[:, sl, 1, un, :] = bim[:, dr, g]
                ccm[:, sl, 0, un, :] = cre[:, dr, g].transpose(0, 2, 1)
                ccm[:, sl, 1, un, :] = cim[:, dr, g].transpose(0, 2, 1)
    o["lamc"], o["bc"], o["cc"] = lamc, bc, ccm
    lamb = np.zeros((depth, 128, 3, 3, 2, 64), f32)
    lamb[:, :, 0] = -0.5
    lamb[:, :, 1] = 1.0
    lamb[:, :, 2] = -3.0
    bb = np.zeros((depth, 128, 2, 3, 2, 64), f32)
    du = np.zeros((depth, 128, 3), f32)
    sd = inp["s5_d"][:depth]
    for g in range(16):
        for cch in range(16):
            ti, p = u_part(g, cch)
            du[:, p, ti] = sd[:, g * 16 + cch]
            for dr in range(2):
                lamb[:, p, 0, ti, dr, :] = lre[:, dr, g, :]
                lamb[:, p, 1, ti, dr, :] = lim[:, dr, g, :]
                lamb[:, p, 2, ti, dr, :] = ldt[:, dr, g][:, None]
                bb[:, p, 0, ti, dr, :] = bre[:, dr, g, :, cch]
                bb[:, p, 1, ti, dr, :] = bim[:, dr, g, :, cch]
    o["lamb"], o["bb"], o["du"] = lamb, bb, du
    return o


CONST_SHAPES = None


def build(L, DEPTH, NSEQ, consts, lp):
    nc = bass.Bass("TRN2", target_bir_lowering=False)
    NCH = L // 128
    NT = L // 512
    NB = L // T1

    def din(name, shape, dt=F32):
        return nc.dram_tensor(name, list(shape), dt, kind="ExternalInput").ap()

    def dscr(name, shape, dt):
        return nc.dram_tensor(name, list(shape), dt, kind="Internal").ap()

    x_in = din("x", [NSEQ, L, D])
    out_d = nc.dram_tensor("out", [NSEQ, L, D], F32, kind="ExternalOutput").ap()
    cd = {k: din("c_" + k, v.shape) for k, v in consts.items()}
    pd = {k: din("p_" + k, v.shape) for k, v in lp.items()}
    wbf = {k: dscr("wbf_" + k, lp[k].shape, BF16) for k in ["w_in2", "wb2", "wa", "wc", "wm", "wo", "wf1", "wf2", "wg"]}
    XT = dscr("XT", [NSEQ, 8, 128, L], F32)
    HS = dscr("HS", [8, 128, L], BF16)
    FMS = {k: dscr("S_" + k, [2, 128, L], BF16) for k in ["rq", "rkt", "gqf", "gkf", "gqb", "gkb"]}
    UTS = dscr("S_ut", [3, 128, L], BF16)
    WZS = dscr("S_wz", [128, 3 * 2 * 8 * 2 * 128], BF16)
    WYS = dscr("S_wy", [128, 16 * 8 * 2 * 32], BF16)
    TPS = dscr("S_tp", [128, 3 * 15 * 128], BF16)
    A8S = dscr("S_a8", [128, 64], F32)
    YTS = dscr("S_yt", [3, 128, L], BF16)
    RV = dscr("S_rv", [L, 256], BF16)
    RGS = dscr("S_rgs", [L, 256], BF16)
    GV = dscr("S_gv", [L, 512], BF16)
    GGS = dscr("S_ggs", [L, 512], BF16)
    KVR = dscr("S_kvr", [NCH, 128, 512], F32)
    KVG = dscr("S_kvg", [NCH, 128, 1024], F32)
    SR = dscr("S_sr", [NCH, 128, 2, 256], BF16)
    SG = dscr("S_sg", [NCH, 128, 2, 512], BF16)

    with ExitStack() as es0:
        P = Prog(nc, es0)
        sp, act, pool, dve, pe = P.sp, P.act, P.pool, P.dve, P.pe

        def cload(key, dt=F32, shape=None):
            src = cd[key]
            shp = list(consts[key].shape)
            t = P.sb(es0, shp, F32, "c_" + key)
            P.dma(sp, t.t[:], src, writes=[t])
            if dt == BF16:
                tb = P.sb(es0, shp, BF16, "cb_" + key)
                P.op(dve, lambda e: e.tensor_copy(out=tb.t[:], in_=t.t[:]), reads=[t], writes=[tb])
                return tb
            return t

        ident_f = cload("ident")
        ident_b = P.sb(es0, [128, 128], BF16, "identb")
        P.op(dve, lambda e: e.tensor_copy(out=ident_b.t[:], in_=ident_f.t[:]), reads=[ident_f], writes=[ident_b])
        ones_b = P.sb(es0, [128, 128], BF16, "onesb")
        P.op(pool, lambda e: e.memset(ones_b.t[:], 1.0), writes=[ones_b])
        eps_t = P.sb(es0, [128, 1], F32, "eps")
        P.op(pool, lambda e: e.memset(eps_t.t[:], EPS), writes=[eps_t])
        one_t = P.sb(es0, [128, 1], F32, "one")
        P.op(pool, lambda e: e.memset(one_t.t[:], 1.0), writes=[one_t])
        gfin = P.sb(es0, [128, 8], F32, "gfin")
        P.dma(sp, gfin.t[:], pd["gfin"], writes=[gfin])

        psf = RR([P.ps(es0, [128, 512], F32, "psf") for _ in range(6)])
        psb = RR([P.ps(es0, [128, 1024], BF16, "psb") for _ in range(2)])

        with ExitStack() as es:
            wst = RR([P.sb(es, [128, 2048], F32, "wst") for _ in range(3)])
            wsb = RR([P.sb(es, [128, 2048], BF16, "wsb") for _ in range(3)])
            cnt = 0
            for k in ([] if NOCAST else wbf):
                shp = lp[k].shape
                for l in range(DEPTH):
                    rows, cols = shp[1], shp[2]
                    for r0 in range(0, rows, 128):
                        r1 = min(rows, r0 + 128)
                        for c0 in range(0, cols, 2048):
                            c1 = min(cols, c0 + 2048)
                            a = wst.get()
                            b = wsb.get()
                            P.dma(sp, a.t[0:r1 - r0, 0:c1 - c0], pd[k][l, r0:r1, c0:c1], writes=[a])
                            eng = [act, dve, pool][cnt % 3]
                            cnt += 1
                            if eng is act:
                                P.op(act, lambda e: e.activation(out=b.t[0:r1 - r0, 0:c1 - c0], in_=a.t[0:r1 - r0, 0:c1 - c0], func=AF.Copy),
                                     reads=[a], writes=[b])
                            else:
                                P.op(eng, lambda e: e.tensor_copy(out=b.t[0:r1 - r0, 0:c1 - c0], in_=a.t[0:r1 - r0, 0:c1 - c0]),
                                     reads=[a], writes=[b])
                            P.dma(sp, wbf[k][l, r0:r1, c0:c1], b.t[0:r1 - r0, 0:c1 - c0], reads=[b])
            P.barrier()

        def rmsnorm(es, xt, gt, outT, sq, rstd):
            P.op(act, lambda e: e.activation(out=sq.t[:], in_=xt.t[:], func=AF.Square), reads=[xt], writes=[sq])
            ps = psf.get()
            P.mm(ps.t[:, :], [(ones_b.t[:], sq.t[:, k, :]) for k in range(8)], reads=[ones_b, sq], writes=[ps])
            P.op(act, lambda e: e.activation(out=rstd.t[:], in_=ps.t[:], func=AF.Sqrt, scale=1.0 / D, bias=eps_t.t[:, 0:1]),
                 reads=[ps, eps_t], writes=[rstd])
            P.op(dve, lambda e: e.reciprocal(out=rstd.t[:], in_=rstd.t[:]), reads=[rstd], writes=[rstd])
            for k in range(8):
                P.op(dve, lambda e, k=k: e.scalar_tensor_tensor(out=outT.t[:, k, :], in0=xt.t[:, k, :], scalar=gt.t[:, k:k + 1],
                                                                in1=rstd.t[:], op0=ALU.mult, op1=ALU.mult),
                     reads=[xt, rstd, gt], writes=[outT])

        def wview(key, l):
            return wbf[key][l].rearrange("(k p) c -> p k c", p=128)

        with ExitStack() as es:
            xin = RR([P.sb(es, [128, D], F32, "xin") for _ in range(2)])
            xst = RR([P.sb(es, [128, 8, 128], F32, "xst") for _ in range(2)])
            for s in range(NSEQ):
                for c in range(NCH):
                    xi = xin.get()
                    P.dma(sp, xi.t[:], x_in[s, c * 128:(c + 1) * 128, :], writes=[xi])
                    xs = xst.get()
                    for half in range(2):
                        ps = psf.get()
                        for kk in range(4):
                            k = half * 4 + kk
                            P.op(pe, lambda e, k=k, kk=kk: e.transpose(out=ps.t[:, kk * 128:(kk + 1) * 128],
                                                                         in_=xi.t[:, k * 128:(k + 1) * 128], identity=ident_f.t[:]),
                                 reads=[xi, ident_f], writes=[ps])
                        P.op(act if half == 0 else dve,
                             (lambda e, half=half: e.activation(out=xs.t[:, half * 4:half * 4 + 4, :], in_=ps.t[:].rearrange("p (k t) -> p k t", k=4), func=AF.Copy))
                             if half == 0 else
                             (lambda e, half=half: e.tensor_copy(out=xs.t[:, half * 4:half * 4 + 4, :], in_=ps.t[:].rearrange("p (k t) -> p k t", k=4))),
                             reads=[ps], writes=[xs])
                    P.dma(sp, XT[s, :, :, c * 128:(c + 1) * 128].rearrange("k p t -> p k t"), xs.t[:], reads=[xs])
            P.barrier()

        wft = cload("wft"); wbt = cload("wbt"); dmat = cload("dmat"); qft = cload("qft"); qbt = cload("qbt"); decr = cload("decr")
        m_le = cload("m_le"); m_ge = cload("m_ge"); m_gt = cload("m_gt"); m_lt = cload("m_lt")
        m_le4 = cload("m_le4"); m_gt4 = cload("m_gt4"); bd_r = cload("bd_r"); bd_g = cload("bd_g")
        maskb = cload("maskb")
        decg = P.sb(es0, [128, NCH, 2, 2], F32, "decg")

        def phase_B2(l, s):
            PI = math.pi
            PB = Buf()

            def V(fn, r=(), w=()):
                P.op(dve, fn, reads=[PB] + list(r), writes=[PB] + list(w))

            def A_(fn):
                P.op(act, fn, reads=[PB], writes=[PB])

            def tt(o, a, b, op):
                V(lambda e: e.tensor_tensor(out=o, in0=a, in1=b, op=op))

            def ts(o, a, s1, op0, s2=None, op1=None):
                if op1 is None:
                    V(lambda e: e.tensor_scalar(out=o, in0=a, scalar1=s1, scalar2=None, op0=op0))
                else:
                    V(lambda e: e.tensor_scalar(out=o, in0=a, scalar1=s1, scalar2=s2, op0=op0, op1=op1))

            def cp(o, a):
                V(lambda e: e.tensor_copy(out=o, in_=a))

            def cmul(ore, oim, are, aim, bre, bim, t1, t2):
                tt(t1, are, bre, ALU.mult)
                tt(t2, aim, bim, ALU.mult)
                tt(ore, t1, t2, ALU.subtract)
                tt(t1, are, bim, ALU.mult)
                tt(t2, aim, bre, ALU.mult)
                tt(oim, t1, t2, ALU.add)

            with ExitStack() as esB:
                WZ = P.sb(esB, [128, 3, 2, 8, 2, 128], BF16, "WZ")
                WY = P.sb(esB, [128, 16, 8, 2, 32], BF16, "WY")
                TAPS = P.sb(esB, [128, 3, 15, 128], BF16, "TAPS")
                A8 = P.sb(esB, [128, 4, 16], F32, "A8")
                for tl in (WZ, WY, TAPS):
                    P.op(pool, lambda e, tl=tl: e.memset(tl.t[:], 0.0), writes=[tl, PB])
                with ExitStack() as es:
                  if s == 0:
                      def lam_stuff(lam3, n, tag):
                          W = P.sb(es, [128, 16, n], F32, "W" + tag)
                          Wi = P.sb(es, [128, n], I32, "Wi" + tag)
                          w = lambda i: W.t[:, i, :]
                          LRE, LIM, LDT = lam3[:, 0, :], lam3[:, 1, :], lam3[:, 2, :]
                          A_(lambda e: e.activation(out=w(0), in_=LDT, func=AF.Exp))
                          tt(w(1), LRE, w(0), ALU.mult)
                          tt(w(2), LIM, w(0), ALU.mult)
                          A_(lambda e: e.activation(out=w(3), in_=w(1), func=AF.Exp))
                          ts(w(4), w(2), 1.0 / (2 * PI), ALU.mult)
                          cp(Wi.t[:], w(4))
                          cp(w(4), Wi.t[:])
                          V(lambda e: e.scalar_tensor_tensor(out=w(2), in0=w(4), scalar=-2 * PI, in1=w(2), op0=ALU.mult, op1=ALU.add))
                          for _ in range(2):
                              ts(w(4), w(2), PI, ALU.is_gt, -2 * PI, ALU.mult)
                              tt(w(2), w(2), w(4), ALU.add)
                              ts(w(4), w(2), -PI, ALU.is_lt, 2 * PI, ALU.mult)
                              tt(w(2), w(2), w(4), ALU.add)
                          ts(w(5), w(2), PI / 2, ALU.add)
                          ts(w(4), w(5), PI, ALU.is_gt, -2 * PI, ALU.mult)
                          tt(w(5), w(5), w(4), ALU.add)
                          ts(w(2), w(2), 3.14159, ALU.min, -3.14159, ALU.max)
                          ts(w(5), w(5), 3.14159, ALU.min, -3.14159, ALU.max)
                          A_(lambda e: e.activation(out=w(6), in_=w(2), func=AF.Sin))
                          A_(lambda e: e.activation(out=w(7), in_=w(5), func=AF.Sin))
                          tt(w(8), w(3), w(7), ALU.mult)
                          tt(w(9), w(3), w(6), ALU.mult)
                          ts(w(12), w(8), -1.0, ALU.add)
                          tt(w(13), LRE, LRE, ALU.mult)
                          tt(w(14), LIM, LIM, ALU.mult)
                          tt(w(13), w(13), w(14), ALU.add)
                          V(lambda e: e.reciprocal(out=w(13), in_=w(13)))
                          tt(w(14), w(12), LRE, ALU.mult)
                          tt(w(15), w(9), LIM, ALU.mult)
                          tt(w(14), w(14), w(15), ALU.add)
                          tt(w(10), w(14), w(13), ALU.mult)
                          tt(w(14), w(9), LRE, ALU.mult)
                          tt(w(15), w(12), LIM, ALU.mult)
                          tt(w(14), w(14), w(15), ALU.subtract)
                          tt(w(11), w(14), w(13), ALU.mult)
                          return W

                      def powers(W, n, npow, tag):
                          Pw = P.sb(es, [128, npow + 1, 2, n], F32, "Pw" + tag)
                          V(lambda e: e.memset(Pw.t[:, 0, 0, :], 1.0))
                          V(lambda e: e.memset(Pw.t[:, 0, 1, :], 0.0))
                          for m in range(npow):
                              cmul(Pw.t[:, m + 1, 0, :], Pw.t[:, m + 1, 1, :], Pw.t[:, m, 0, :], Pw.t[:, m, 1, :],
                                   W.t[:, 8, :], W.t[:, 9, :], W.t[:, 14, :], W.t[:, 15, :])
                          return Pw

                      lamc = P.sb(es, [128, 3, 16], F32, "lamc")
                      bct = P.sb(es, [128, 2, 16, 16], F32, "bct")
                      cct = P.sb(es, [128, 2, 16, 16], F32, "cct")
                      P.dma(sp, lamc.t[:], pd["lamc"][l], writes=[lamc, PB])
                      P.dma(sp, bct.t[:], pd["bc"][l], writes=[bct, PB])
                      P.dma(sp, cct.t[:], pd["cc"][l], writes=[cct, PB])
                      Wc = lam_stuff(lamc.t, 16, "c")
                      Pc = powers(Wc, 16, 8, "c")
                      A_(lambda e: e.activation(out=A8.t[:, 0, :], in_=Wc.t[:, 1, :], func=AF.Exp, scale=8.0))
                      A_(lambda e: e.activation(out=A8.t[:, 3, :], in_=Wc.t[:, 1, :], func=AF.Exp, scale=-8.0))
                      tt(A8.t[:, 1, :], Pc.t[:, 8, 0, :], A8.t[:, 3, :], ALU.mult)
                      tt(A8.t[:, 2, :], Pc.t[:, 8, 1, :], A8.t[:, 3, :], ALU.mult)

                      def bc16(ap2):
                          return ap2.unsqueeze(2).broadcast_to([128, 16, 16])
                      T3 = P.sb(es, [128, 6, 16, 16], F32, "T3")
                      Bb = P.sb(es, [128, 2, 16, 16], F32, "Bb")
                      cmul(Bb.t[:, 0], Bb.t[:, 1], bct.t[:, 0], bct.t[:, 1], bc16(Wc.t[:, 10, :]), bc16(Wc.t[:, 11, :]), T3.t[:, 0], T3.t[:, 1])
                      CD = P.sb(es, [128, 16, 2, 32], F32, "CD")
                      BPD = P.sb(es, [128, 16, 2, 32], F32, "BPD")
                      V(lambda e: e.memset(CD.t[:], 0.0))
                      V(lambda e: e.memset(BPD.t[:], 0.0))
                      cp(CD.t[0:64, :, 0, 0:16], cct.t[0:64, 0])
                      cp(CD.t[64:128, :, 0, 16:32], cct.t[64:128, 0])
                      ts(CD.t[0:64, :, 1, 0:16], cct.t[0:64, 1], -1.0, ALU.mult)
                      ts(CD.t[64:128, :, 1, 16:32], cct.t[64:128, 1], -1.0, ALU.mult)
                      dut = P.sb(es, [128, 3], F32, "dut")
                      P.dma(sp, dut.t[:], pd["du"][l], writes=[dut, PB])
                      DD = P.sb(es, [128, 3, 128], F32, "DD")
                      for ti in range(3):
                          ts(DD.t[:, ti, :], ident_f.t[:], dut.t[:, ti:ti + 1], ALU.mult)
                      pst = [psf.get() for _ in range(3)]
                      for tau in range(8):
                          cmul(T3.t[:, 2], T3.t[:, 3], Bb.t[:, 0], Bb.t[:, 1], bc16(Pc.t[:, tau, 0, :]), bc16(Pc.t[:, tau, 1, :]), T3.t[:, 0], T3.t[:, 1])
                          for ri in range(2):
                              cp(BPD.t[0:64, :, ri, 0:16], T3.t[0:64, 2 + ri])
                              cp(BPD.t[64:128, :, ri, 16:32], T3.t[64:128, 2 + ri])
                          for gp in range(8):
                              ti, sl = gp // 3, gp % 3
                              rs = slice(32 * sl, 32 * sl + 32)
                              for dr in range(2):
                                  un = gp * 2 + dr
                                  k = 7 + tau if dr == 0 else 7 - tau
                                  if tau == 0 and dr == 1:
                                      continue
                                  pairs = []
                                  if tau == 0:
                                      pairs.append((DD.t[rs, ti, 32 * sl:32 * sl + 32], ident_f.t[rs, 32 * sl:32 * sl + 32]))
                                      for d2 in range(2):
                                          u2 = gp * 2 + d2
                                          pairs += [(BPD.t[:, u2, 0, :], CD.t[:, u2, 0, :]), (BPD.t[:, u2, 1, :], CD.t[:, u2, 1, :])]
                                  else:
                                      pairs = [(BPD.t[:, un, 0, :], CD.t[:, un, 0, :]), (BPD.t[:, un, 1, :], CD.t[:, un, 1, :])]
                                  P.mm(pst[ti].t[rs, k * 32:(k + 1) * 32], pairs, reads=[PB, ident_f], writes=[pst[ti]])
                      for gp in range(8):
                          ti, sl = gp // 3, gp % 3
                          rs = slice(32 * sl, 32 * sl + 32)
                          V(lambda e: e.tensor_copy(out=TAPS.t[rs, ti, :, 32 * sl:32 * sl + 32],
                                                    in_=pst[ti].t[rs, 0:480].rearrange("p (k c) -> p k c", c=32)), r=[pst[ti]], w=[TAPS])
                      PWi = P.sb(es, [128, 2, 16], F32, "PWi")
                      for i in range(8):
                          for ri in range(2):
                              cp(PWi.t[:, ri, :].rearrange("p (g d) -> p g d", d=2)[:, :, 0], Pc.t[:, i + 1, ri, :].rearrange("p (g d) -> p g d", d=2)[:, :, 0])
                              cp(PWi.t[:, ri, :].rearrange("p (g d) -> p g d", d=2)[:, :, 1], Pc.t[:, 8 - i, ri, :].rearrange("p (g d) -> p g d", d=2)[:, :, 1])
                          cmul(T3.t[:, 2], T3.t[:, 3], cct.t[:, 0], cct.t[:, 1], bc16(PWi.t[:, 0, :]), bc16(PWi.t[:, 1, :]), T3.t[:, 0], T3.t[:, 1])
                          V(lambda e: e.tensor_copy(out=WY.t[0:64, :, i, 0, 0:16], in_=T3.t[0:64, 2]), w=[WY])
                          V(lambda e: e.tensor_copy(out=WY.t[64:128, :, i, 0, 16:32], in_=T3.t[64:128, 2]), w=[WY])
                          V(lambda e: e.tensor_scalar(out=WY.t[0:64, :, i, 1, 0:16], in0=T3.t[0:64, 3], scalar1=-1.0, scalar2=None, op0=ALU.mult), w=[WY])
                          V(lambda e: e.tensor_scalar(out=WY.t[64:128, :, i, 1, 16:32], in0=T3.t[64:128, 3], scalar1=-1.0, scalar2=None, op0=ALU.mult), w=[WY])
                      lamb = P.sb(es, [128, 3, 384], F32, "lamb")
                      bbt = P.sb(es, [128, 2, 384], F32, "bbt")
                      P.dma(sp, lamb.t[:], pd["lamb"][l].rearrange("p w t d q -> p w (t d q)"), writes=[lamb, PB])
                      P.dma(sp, bbt.t[:], pd["bb"][l].rearrange("p w t d q -> p w (t d q)"), writes=[bbt, PB])
                      Wb = lam_stuff(lamb.t, 384, "b")
                      Pb = powers(Wb, 384, 7, "b")
                      Bbb = P.sb(es, [128, 2, 384], F32, "Bbb")
                      T4 = P.sb(es, [128, 6, 384], F32, "T4")
                      cmul(Bbb.t[:, 0], Bbb.t[:, 1], bbt.t[:, 0], bbt.t[:, 1], Wb.t[:, 10, :], Wb.t[:, 11, :], T4.t[:, 0], T4.t[:, 1])

                      def tdq(ap2):
                          return ap2.rearrange("p (t d q) -> p t d q", t=3, d=2)
                      for j in range(8):
                          for ri in range(2):
                              cp(tdq(T4.t[:, 2 + ri, :])[:, :, 0, :], tdq(Pb.t[:, 7 - j, ri, :])[:, :, 0, :])
                              cp(tdq(T4.t[:, 2 + ri, :])[:, :, 1, :], tdq(Pb.t[:, j, ri, :])[:, :, 1, :])
                          cmul(T4.t[:, 4], T4.t[:, 5], T4.t[:, 2], T4.t[:, 3], Bbb.t[:, 0], Bbb.t[:, 1], T4.t[:, 0], T4.t[:, 1])
                          for ri in range(2):
                              for g2 in range(2):
                                  mk = maskb.t[:, g2 * 64:(g2 + 1) * 64].unsqueeze(1).unsqueeze(1).broadcast_to([128, 3, 2, 64])
                                  V(lambda e: e.tensor_tensor(out=WZ.t[:, :, :, j, ri, g2 * 64:(g2 + 1) * 64], in0=tdq(T4.t[:, 4 + ri, :]), in1=mk, op=ALU.mult),
                                    r=[maskb], w=[WZ])
                      P.dma(sp, WZS, WZ.t[:].rearrange("p a b c d e -> p (a b c d e)"), reads=[WZ, PB])
                      P.dma(sp, WYS, WY.t[:].rearrange("p a b c d -> p (a b c d)"), reads=[WY, PB])
                      P.dma(sp, TPS, TAPS.t[:].rearrange("p a b c -> p (a b c)"), reads=[TAPS, PB])
                      P.dma(sp, A8S, A8.t[:].rearrange("p a b -> p (a b)"), reads=[A8, PB])
                  else:
                    P.dma(sp, WZ.t[:].rearrange("p a b c d e -> p (a b c d e)"), WZS, writes=[WZ, PB])
                    P.dma(sp, WY.t[:].rearrange("p a b c d -> p (a b c d)"), WYS, writes=[WY, PB])
                    P.dma(sp, TAPS.t[:].rearrange("p a b c -> p (a b c)"), TPS, writes=[TAPS, PB])
                    P.dma(sp, A8.t[:].rearrange("p a b -> p (a b)"), A8S, writes=[A8, PB])
                  P.barrier()
                NBp = NB + 1
                uTs = P.sb(esB, [128, 3, L], BF16, "uTs")
                P.dma(sp, uTs.t[:], UTS.rearrange("k p t -> p k t"), writes=[uTs])
                Xbf = P.sb(esB, [128, 16, 2, NB], BF16, "Xbf")
                with ExitStack() as es:
                    zb = [P.sb(es, [128, 4, NB], F32, "zb%d" % i) for i in range(4)]
                    Er = P.sb(es, [128, 4, NBp], F32, "Er")
                    Ei = P.sb(es, [128, 4, NBp], F32, "Ei")
                    RHO = P.sb(es, [128, 4, NB], F32, "RHO")
                    Uw = P.sb(es, [128, 2, 2, 4], F32, "Uw")
                    Ut = P.sb(es, [128, 2, 4], F32, "Ut")
                    for dr in range(2):
                        for hf in range(2):
                            uns = [(4 * hf + g) * 2 + dr for g in range(4)]
                            usl = A8.t[:, :, :].rearrange("p w (g d) -> p w g d", d=2)[:, :, 4 * hf:4 * hf + 4, dr]
                            V(lambda e: e.memset(Er.t[:, :, 0:1], 1.0))
                            V(lambda e: e.memset(Ei.t[:, :, 0:1], 0.0))
                            cp(Uw.t[:, 0, 0, :], usl[:, 1, :])
                            cp(Uw.t[:, 0, 1, :], usl[:, 2, :])
                            k = 0
                            cur = 0
                            while (1 << k) <= NB:
                                lo = 1 << k
                                hi = min(lo * 2, NBp)
                                wd = hi - lo
                                ub_r = Uw.t[:, cur, 0, :].unsqueeze(2).broadcast_to([128, 4, wd])
                                ub_i = Uw.t[:, cur, 1, :].unsqueeze(2).broadcast_to([128, 4, wd])
                                cmul(Er.t[:, :, lo:hi], Ei.t[:, :, lo:hi], Er.t[:, :, 0:wd], Ei.t[:, :, 0:wd], ub_r, ub_i, zb[0].t[:, :, 0:wd], zb[1].t[:, :, 0:wd])
                                cmul(Uw.t[:, 1 - cur, 0, :], Uw.t[:, 1 - cur, 1, :], Uw.t[:, cur, 0, :], Uw.t[:, cur, 1, :],
                                     Uw.t[:, cur, 0, :], Uw.t[:, cur, 1, :], Ut.t[:, 0, :], Ut.t[:, 1, :])
                                cur = 1 - cur
                                k += 1
                            cp(RHO.t[:], usl[:, 0, :].unsqueeze(2).broadcast_to([128, 4, NB]))
                            for g in range(4):
                                gp = 4 * hf + g
                                ti, sl = gp // 3, gp % 3
                                rs = slice(32 * sl, 32 * sl + 32)
                                for ri in range(2):
                                    ps = psf.get()
                                    pairs = []
                                    for j in range(8):
                                        if dr == 0:
                                            rhs = uTs.t[rs, ti, j:L:8]
                                        else:
                                            rhs = uTs.t[rs, ti, j:L:8][:, ::-1]
                                        pairs.append((WZ.t[rs, ti, dr, j, ri, :], rhs))
                                    P.mm(ps.t[:, 0:NB], pairs, reads=[WZ, uTs], writes=[ps])
                                    if ri == 0:
                                        P.op(act, lambda e: e.activation(out=zb[0].t[:, g, :], in_=ps.t[:, 0:NB], func=AF.Copy), reads=[ps, PB], writes=[PB])
                                    else:
                                        V(lambda e: e.tensor_copy(out=zb[1].t[:, g, :], in_=ps.t[:, 0:NB]), r=[ps])
                            zr, zi, t1, t2 = zb[0].t, zb[1].t, zb[2].t, zb[3].t
                            c1, s1 = Er.t[:, :, 1:NBp], Ei.t[:, :, 1:NBp]
                            tt(t1[:], zr[:], c1, ALU.mult)
                            tt(t2[:], zi[:], s1, ALU.mult)
                            tt(t1[:], t1[:], t2[:], ALU.add)
                            tt(t2[:], zr[:], s1, ALU.mult)
                            tt(zr[:], zi[:], c1, ALU.mult)
                            tt(zr[:], zr[:], t2[:], ALU.subtract)
                            for g in range(4):
                                V(lambda e: e.tensor_tensor_scan(out=zi[:, g, :], data0=RHO.t[:, g, :], data1=t1[:, g, :], initial=0.0, op0=ALU.mult, op1=ALU.add))
                                V(lambda e: e.tensor_tensor_scan(out=t2[:, g, :], data0=RHO.t[:, g, :], data1=zr[:, g, :], initial=0.0, op0=ALU.mult, op1=ALU.add))
                            xr, xi = zi, t2
                            c0, s0 = Er.t[:, :, 1:NB], Ei.t[:, :, 1:NB]
                            xbr = Xbf.t[:, :, 0, :].rearrange("p (g d) n -> p g d n", d=2)[:, 4 * hf:4 * hf + 4, dr, :]
                            xbi = Xbf.t[:, :, 1, :].rearrange("p (g d) n -> p g d n", d=2)[:, 4 * hf:4 * hf + 4, dr, :]
                            V(lambda e: e.memset(xbr[:, :, 0:1], 0.0), w=[Xbf])
                            V(lambda e: e.memset(xbi[:, :, 0:1], 0.0), w=[Xbf])
                            tt(t1[:, :, 0:NB - 1], xr[:, :, 0:NB - 1], c0, ALU.mult)
                            tt(zr[:, :, 0:NB - 1], xi[:, :, 0:NB - 1], s0, ALU.mult)
                            V(lambda e: e.tensor_tensor(out=xbr[:, :, 1:NB], in0=t1[:, :, 0:NB - 1], in1=zr[:, :, 0:NB - 1], op=ALU.subtract), w=[Xbf])
                            tt(t1[:, :, 0:NB - 1], xr[:, :, 0:NB - 1], s0, ALU.mult)
                            tt(zr[:, :, 0:NB - 1], xi[:, :, 0:NB - 1], c0, ALU.mult)
                            V(lambda e: e.tensor_tensor(out=xbi[:, :, 1:NB], in0=t1[:, :, 0:NB - 1], in1=zr[:, :, 0:NB - 1], op=ALU.add), w=[Xbf])
                    P.barrier()
                with ExitStack() as es:
                    yt_p = RR([P.sb(es, [128, 3, 512], BF16, "ytp") for _ in range(2)])
                    for tix in range(NT):
                        t0 = tix * 512
                        nb0 = t0 // 8
                        yt = yt_p.get()
                        for ti in range(3):
                            ps = psf.get()
                            pv = ps.t[:, 0:512].rearrange("p (n i) -> p n i", i=8)
                            uv = uTs.t[:, ti, t0:t0 + 512].rearrange("p (n i) -> p n i", i=8)
                            mms = [(ps.t[:, 0:512], TAPS.t[0:64, ti, 7, :], uTs.t[0:64, ti, t0:t0 + 512])]
                            for tau in range(1, 8):
                                for i in range(tau, 8):
                                    mms.append((pv[:, :, i], TAPS.t[:, ti, 7 + tau, :], uv[:, :, i - tau]))
                                for i in range(0, 8 - tau):
                                    mms.append((pv[:, :, i], TAPS.t[:, ti, 7 - tau, :], uv[:, :, i + tau]))
                            npairs = 3 if ti < 2 else 2
                            for sl in range(npairs):
                                gp = 3 * ti + sl
                                rs = slice(32 * sl, 32 * sl + 32)
                                pvs = ps.t[rs, 0:512].rearrange("p (n i) -> p n i", i=8)
                                for dr in range(2):
                                    un = gp * 2 + dr
                                    for i in range(8):
                                        for ri in range(2):
                                            if dr == 0:
                                                rhs = Xbf.t[:, un, ri, nb0:nb0 + 64]
                                            else:
                                                a0 = NB - 1 - nb0
                                                rhs = Xbf.t[:, un, ri, a0 - 63:a0 + 1][:, ::-1]
                                            mms.append((pvs[:, :, i], WY.t[:, un, i, ri, :], rhs))
                            mms.append((ps.t[:, 0:512], TAPS.t[64:128, ti, 7, :], uTs.t[64:128, ti, t0:t0 + 512]))
                            nm = len(mms)

                            def emit(e):
                                ins = None
                                for ix, (o, lh, rh) in enumerate(mms):
                                    ins = e.matmul(o, lhsT=lh, rhs=rh, start=(ix == 0), stop=(ix == nm - 1))
                                return ins
                            P.op(pe, emit, reads=[TAPS, WY, Xbf, uTs], writes=[ps])
                            P.op(act, lambda e: e.activation(out=yt.t[:, ti, :], in_=ps.t[:, 0:512], func=AF.Gelu), reads=[ps], writes=[yt])
                        P.dma(sp, YTS[:, :, t0:t0 + 512].rearrange("k p t -> p k t"), yt.t[:], reads=[yt])
                    P.barrier()

        def fm_store(key, tile_T, s, t0, n=512):
            P.dma(sp, FMS[key][:, :, t0:t0 + n].rearrange("k p t -> p k t"), tile_T.t[:, :, 0:n], reads=[tile_T])

        WCOLS = {"fm0": (0, 512), "fm1": (512, 512), "fm2": (1024, 512), "fm3": (1536, 416),
                 "tm0": (NFM, 512), "tm1": (NFM + 512, 512), "tm2": (NFM + 1024, 512), "tm3": (NFM + 1536, 512),
                 "tm4": (NFM + 2048, 256)}

        def phase_A(l, s):
            with ExitStack() as es:
                wpool = RR([P.sb(es, [128, 8, 512], BF16, "wA") for _ in range(4)])
                xt = P.sb(es, [128, 8, 512], F32, "xA")
                sq = P.sb(es, [128, 8, 512], BF16, "sqA")
                rstd = P.sb(es, [128, 512], F32, "rstdA")
                h = P.sb(es, [128, 8, 512], BF16, "hA")
                g1 = P.sb(es, [128, 8], F32, "g1")
                P.dma(sp, g1.t[:], pd["g1"][l], writes=[g1])
                tabs = {k: P.sb(es, [128, 512], F32, "tab" + k) for k in ["cosT", "sinT", "cosT8", "sinT8"]}
                tmA = RR([P.sb(es, [128, 512], F32, "tmA") for _ in range(6)])
                qT = P.sb(es, [128, 2, 512], BF16, "qT")
                kT = P.sb(es, [128, 2, 512], BF16, "kT")
                gq = P.sb(es, [128, 2, 512], F32, "gq")
                gk = P.sb(es, [128, 2, 512], F32, "gk")
                qf = P.sb(es, [128, 2, 512], BF16, "qf")
                kf = P.sb(es, [128, 2, 512], BF16, "kf")
                qb = P.sb(es, [128, 2, 512], BF16, "qb")
                kb = P.sb(es, [128, 2, 512], BF16, "kb")
                uT = P.sb(es, [128, 3, 512], BF16, "uTa")
                lrT = P.sb(es, [32, 512], BF16, "lrT")
                P.op(pool, lambda e: e.memset(lrT.t[:], 1.0), writes=[lrT])
                wgt = P.sb(es, [32, 512], BF16, "wgt")
                P.dma(sp, wgt.t[:], wbf["wg"][l], writes=[wgt])
                cm = P.sb(es, [128, 4, 256], F32, "cm")
                sm = P.sb(es, [128, 4, 256], F32, "sm")
                rng = P.sb(es, [128, 256], F32, "rng")
                gng = P.sb(es, [128, 512], F32, "gng")
                P.dma(sp, rng.t[:], pd["rng"][l], writes=[rng])
                P.dma(sp, gng.t[:], pd["gng"][l], writes=[gng])
                krf = P.sb(es, [128, 4, 256], BF16, "krf")
                krb = P.sb(es, [128, 4, 256], BF16, "krb")
                vt = P.sb(es, [128, 4, 256], BF16, "vt")
                rgs = P.sb(es, [128, 4, 256], BF16, "rgs")
                gvt = P.sb(es, [128, 4, 512], BF16, "gvt")
                ggs = P.sb(es, [128, 4, 512], BF16, "ggs")
                gkt = P.sb(es, [128, 4, 256], F32, "gkt")
                lp_t = P.sb(es, [128, 512], F32, "lp")
                kstf = P.sb(es, [128, 256], BF16, "kstf")
                kstb = P.sb(es, [128, 256], BF16, "kstb")
                kvr_t = RR([P.sb(es, [128, 512], F32, "kvr") for _ in range(2)])
                kvg_t = RR([P.sb(es, [128, 1024], F32, "kvg") for _ in range(2)])
                for ti in range(NT):
                    t0 = ti * 512
                    P.dma(sp, xt.t[:], XT[s, :, :, t0:t0 + 512].rearrange("k p t -> p k t"), writes=[xt])
                    for k in tabs:
                        P.dma(sp, tabs[k].t[:], cd[k][:, t0:t0 + 512], writes=[tabs[k]])
                    P.dma(sp, cm.t[:], cd["cosM"][t0:t0 + 512, :].rearrange("(c p) f -> p c f", p=128), writes=[cm])
                    P.dma(sp, sm.t[:], cd["sinM"][t0:t0 + 512, :].rearrange("(c p) f -> p c f", p=128), writes=[sm])
                    rmsnorm(es, xt, g1, h, sq, rstd)
                    P.dma(sp, HS[:, :, t0:t0 + 512].rearrange("k p t -> p k t"), h.t[:], reads=[h])

                    def wload(name):
                        c0, n = WCOLS[name]
                        w = wpool.get()
                        P.dma(sp, w.t[:, :, 0:n], wview("w_in2", l)[:, :, c0:c0 + n], writes=[w])
                        return w

                    def fmm(w, f, m=128):
                        ps = psf.get()
                        P.mm(ps.t[0:m, :], [(w.t[:, k, f * 128:f * 128 + m], h.t[:, k, :]) for k in range(8)], reads=[w, h], writes=[ps])
                        return ps
                    for name, dst, ct, st in (("fm0", qT, "cosT8", "sinT8"), ("fm1", kT, "cosT", "sinT")):
                        w = wload(name)
                        for hp in range(2):
                            pa = fmm(w, 2 * hp)
                            pb = fmm(w, 2 * hp + 1)
                            a = tmA.get()
                            b = tmA.get()
                            P.op(dve, lambda e: e.tensor_tensor(out=a.t[:], in0=pa.t[:], in1=tabs[ct].t[:], op=ALU.mult), reads=[pa, tabs[ct]], writes=[a])
                            P.op(dve, lambda e: e.tensor_tensor(out=b.t[:], in0=pb.t[:], in1=tabs[st].t[:], op=ALU.mult), reads=[pb, tabs[st]], writes=[b])
                            P.op(pool, lambda e: e.tensor_tensor(out=dst.t[:, hp, :], in0=a.t[:], in1=b.t[:], op=ALU.add), reads=[a, b], writes=[dst])
                    fm_store("rq", qT, s, t0)
                    fm_store("rkt", kT, s, t0)
                    w = wload("fm2")
                    for f in range(4):
                        ps = fmm(w, f)
                        dst = gq if f < 2 else gk
                        P.op(act, lambda e: e.activation(out=dst.t[:, f % 2, :], in_=ps.t[:], func=AF.Copy), reads=[ps], writes=[dst])
                    w = wload("fm3")
                    for f in range(3):
                        ps = fmm(w, f)
                        P.op(act, lambda e: e.activation(out=uT.t[:, f, :], in_=ps.t[:], func=AF.Copy), reads=[ps], writes=[uT])
                    P.dma(sp, UTS[:, :, t0:t0 + 512].rearrange("k p t -> p k t"), uT.t[:], reads=[uT])
                    ps = fmm(w, 3, m=16)
                    P.op(act, lambda e: e.activation(out=lrT.t[0:16, :], in_=ps.t[0:16, :], func=AF.Copy), reads=[ps], writes=[lrT])
                    def tmm(w, c, n):
                        ps = psf.get()
                        P.mm(ps.t[:, 0:n], [(h.t[:, k, c * 128:(c + 1) * 128], w.t[:, k, 0:n]) for k in range(8)], reads=[w, h], writes=[ps])
                        return ps
                    w = wload("tm0")
                    for c in range(4):
                        ps = tmm(w, c, 512)
                        a = tmA.get()
                        b = tmA.get()
                        P.op(dve, lambda e: e.tensor_tensor(out=a.t[:, 0:256], in0=ps.t[:, 0:256], in1=cm.t[:, c, :], op=ALU.mult), reads=[ps, cm], writes=[a])
                        P.op(dve, lambda e: e.tensor_tensor(out=b.t[:, 0:256], in0=ps.t[:, 256:512], in1=sm.t[:, c, :], op=ALU.mult), reads=[ps, sm], writes=[b])
                        P.op(pool, lambda e: e.tensor_tensor(out=a.t[:, 0:256], in0=a.t[:, 0:256], in1=b.t[:, 0:256], op=ALU.add), reads=[a, b], writes=[a])
                        P.op(pool, lambda e: e.tensor_tensor(out=krf.t[:, c, :], in0=a.t[:, 0:256], in1=wft.t[:], op=ALU.mult), reads=[a, wft], writes=[krf])
                        P.op(pool, lambda e: e.tensor_tensor(out=krb.t[:, c, :], in0=a.t[:, 0:256], in1=wbt.t[:], op=ALU.mult), reads=[a, wbt], writes=[krb])
                    w = wload("tm1")
                    for c in range(4):
                        ps = tmm(w, c, 512)
                        a = tmA.get()
                        P.op(act, lambda e: e.activation(out=vt.t[:, c, :], in_=ps.t[:, 0:256], func=AF.Copy), reads=[ps], writes=[vt])
                        P.op(act, lambda e: e.activation(out=a.t[:, 0:256], in_=ps.t[:, 256:512], func=AF.Silu), reads=[ps], writes=[a])
                        P.op(pool, lambda e: e.tensor_tensor(out=rgs.t[:, c, :], in0=a.t[:, 0:256], in1=rng.t[:], op=ALU.mult), reads=[a, rng], writes=[rgs])
                    P.dma(sp, RV[t0:t0 + 512, :].rearrange("(c p) f -> p c f", p=128), vt.t[:], reads=[vt])
                    P.dma(sp, RGS[t0:t0 + 512, :].rearrange("(c p) f -> p c f", p=128), rgs.t[:], reads=[rgs])
                    w = wload("tm2")
                    for c in range(4):
                        ps = tmm(w, c, 512)
                        P.op(act, lambda e: e.activation(out=gvt.t[:, c, :], in_=ps.t[:], func=AF.Copy), reads=[ps], writes=[gvt])
                    P.dma(sp, GV[t0:t0 + 512, :].rearrange("(c p) f -> p c f", p=128), gvt.t[:], reads=[gvt])
                    w = wload("tm3")
                    for c in range(4):
                        ps = tmm(w, c, 512)
                        a = tmA.get()
                        P.op(act, lambda e: e.activation(out=a.t[:], in_=ps.t[:], func=AF.Silu), reads=[ps], writes=[a])
                        P.op(pool, lambda e: e.tensor_tensor(out=ggs.t[:, c, :], in0=a.t[:], in1=gng.t[:], op=ALU.mult), reads=[a, gng], writes=[ggs])
                    P.dma(sp, GGS[t0:t0 + 512, :].rearrange("(c p) f -> p c f", p=128), ggs.t[:], reads=[ggs])
                    w = wload("tm4")
                    for c in range(4):
                        ps = tmm(w, c, 256)
                        P.op(act, lambda e: e.activation(out=gkt.t[:, c, :], in_=ps.t[:, 0:256], func=AF.Copy), reads=[ps], writes=[gkt])
                    for c in range(4):
                        ch = ti * 4 + c
                        cs = slice(c * 128, (c + 1) * 128)
                        ps = psf.get()
                        P.mm(ps.t[:, :], [(lrT.t[0:32, cs], wgt.t[0:32, :])], reads=[lrT, wgt], writes=[ps])
                        a = tmA.get()
                        P.op(act, lambda e: e.activation(out=a.t[:], in_=ps.t[:], func=AF.Exp, scale=-1.0), reads=[ps], writes=[a])
                        P.op(act, lambda e: e.activation(out=lp_t.t[:], in_=a.t[:], func=AF.Ln, bias=one_t.t[:, 0:1]), reads=[a, one_t], writes=[lp_t])
                        ps = psf.get()
                        for dr in range(2):
                            for hp in range(2):
                                blk = dr * 2 + hp
                                mk = m_le if dr == 0 else m_ge
                                P.mm(ps.t[:, blk * 128:(blk + 1) * 128], [(lp_t.t[:, dr * 256 + hp * 128: dr * 256 + (hp + 1) * 128], mk.t[:])],
                                     reads=[lp_t, mk], writes=[ps])
                        en = tmA.get()
                        ep = tmA.get()
                        P.op(act, lambda e: e.activation(out=en.t[:], in_=ps.t[:], func=AF.Exp, scale=-1.0 / 16), reads=[ps], writes=[en])
                        P.op(act, lambda e: e.activation(out=ep.t[:], in_=ps.t[:], func=AF.Exp, scale=1.0 / 16), reads=[ps], writes=[ep])
                        for dr in range(2):
                            for hp in range(2):
                                blk = dr * 2 + hp
                                bs = slice(blk * 128, (blk + 1) * 128)
                                qd = qf if dr == 0 else qb
                                kd = kf if dr == 0 else kb
                                P.op(dve, lambda e: e.scalar_tensor_tensor(out=qd.t[:, hp, cs], in0=gq.t[:, hp, cs], scalar=0.125, in1=en.t[:, bs],
                                                                           op0=ALU.mult, op1=ALU.mult), reads=[gq, en], writes=[qd])
                                P.op(pool, lambda e: e.tensor_tensor(out=kd.t[:, hp, cs], in0=gk.t[:, hp, cs], in1=ep.t[:, bs], op=ALU.mult),
                                     reads=[gk, ep], writes=[kd])
                        env = en.t[:].rearrange("p (d h i) -> p d h i", d=2, h=2)
                        P.op(pool, lambda e: e.tensor_copy(out=decg.t[:, ch, 0, :], in_=env[:, 0, :, 127]), reads=[en], writes=[decg])
                        P.op(pool, lambda e: e.tensor_copy(out=decg.t[:, ch, 1, :], in_=env[:, 1, :, 0]), reads=[en], writes=[decg])
                        ps = psf.get()
                        P.mm(ps.t[:, 0:256], [(m_gt.t[:], lp_t.t[:, 0:256])], reads=[lp_t, m_gt], writes=[ps])
                        P.mm(ps.t[:, 256:512], [(m_lt.t[:], lp_t.t[:, 256:512])], reads=[lp_t, m_lt], writes=[ps])
                        a = tmA.get()
                        P.op(act, lambda e: e.activation(out=a.t[:], in_=ps.t[:], func=AF.Exp, scale=-1.0 / 16), reads=[ps], writes=[a])
                        P.op(dve, lambda e: e.tensor_tensor(out=kstf.t[:], in0=gkt.t[:, c, :], in1=a.t[:, 0:256], op=ALU.mult), reads=[gkt, a], writes=[kstf])
                        P.op(pool, lambda e: e.tensor_tensor(out=kstb.t[:], in0=gkt.t[:, c, :], in1=a.t[:, 256:512], op=ALU.mult), reads=[gkt, a], writes=[kstb])
                        ps = psf.get()
                        for dr in range(2):
                            for hp in range(2):
                                blk = dr * 2 + hp
                                kk = krf if dr == 0 else krb
                                P.mm(ps.t[:, blk * 128:(blk + 1) * 128], [(kk.t[:, c, hp * 128:(hp + 1) * 128], vt.t[:, c, hp * 128:(hp + 1) * 128])],
                                     reads=[kk, vt], writes=[ps])
                        kv = kvr_t.get()
                        P.op(act, lambda e: e.activation(out=kv.t[:], in_=ps.t[:], func=AF.Copy), reads=[ps], writes=[kv])
                        P.dma(sp, KVR[ch], kv.t[:], reads=[kv])
                        kvg = kvg_t.get()
                        for dr in range(2):
                            ps = psf.get()
                            kk = kstf if dr == 0 else kstb
                            for hp in range(2):
                                P.mm(ps.t[:, hp * 256:(hp + 1) * 256], [(kk.t[:, hp * 128:(hp + 1) * 128], gvt.t[:, c, hp * 256:(hp + 1) * 256])],
                                     reads=[kk, gvt], writes=[ps])
                            P.op(act if dr == 0 else dve,
                                 (lambda e: e.activation(out=kvg.t[:, 0:512], in_=ps.t[:], func=AF.Copy)) if dr == 0 else
                                 (lambda e: e.tensor_copy(out=kvg.t[:, 512:1024], in_=ps.t[:])), reads=[ps], writes=[kvg])
                        P.dma(sp, KVG[ch], kvg.t[:], reads=[kvg])
                    for key, tl in (("gqf", qf), ("gkf", kf), ("gqb", qb), ("gkb", kb)):
                        fm_store(key, tl, s, t0)
                P.barrier()

        def phase_B(l, s):
            with ExitStack() as es:
                kvr_in = RR([P.sb(es, [128, 512], F32, "kvri") for _ in range(4)])
                kvg_in = RR([P.sb(es, [128, 1024], F32, "kvgi") for _ in range(4)])
                srs = P.sb(es, [128, 2, 256], F32, "srs")
                sgs = P.sb(es, [128, 2, 512], F32, "sgs")
                P.op(pool, lambda e: e.memset(srs.t[:], 0.0), writes=[srs])
                P.op(pool, lambda e: e.memset(sgs.t[:], 0.0), writes=[sgs])
                sro = RR([P.sb(es, [128, 256], BF16, "sro") for _ in range(4)])
                sgo = RR([P.sb(es, [128, 512], BF16, "sgo") for _ in range(4)])
                for step in range(NCH):
                    for dr in range(2):
                        n = step if dr == 0 else NCH - 1 - step
                        kr = kvr_in.get()
                        kg = kvg_in.get()
                        P.dma(sp, kr.t[:, 0:256], KVR[n, :, dr * 256:(dr + 1) * 256], writes=[kr])
                        P.dma(sp, kg.t[:, 0:512], KVG[n, :, dr * 512:(dr + 1) * 512], writes=[kg])
                        o1 = sro.get()
                        o2 = sgo.get()
                        P.op(pool, lambda e: e.tensor_tensor(out=o1.t[:], in0=srs.t[:, dr, :], in1=bd_r.t[:], op=ALU.mult), reads=[srs, bd_r], writes=[o1])
                        P.op(pool, lambda e: e.tensor_tensor(out=o2.t[:], in0=sgs.t[:, dr, :], in1=bd_g.t[:], op=ALU.mult), reads=[sgs, bd_g], writes=[o2])
                        P.dma(sp, SR[n, :, dr, :], o1.t[:], reads=[o1])
                        P.dma(sp, SG[n, :, dr, :], o2.t[:], reads=[o2])
                        for hp in range(2):
                            P.op(dve, lambda e: e.scalar_tensor_tensor(out=srs.t[:, dr, hp * 128:(hp + 1) * 128], in0=srs.t[:, dr, hp * 128:(hp + 1) * 128],
                                                                       scalar=decr.t[:, hp:hp + 1], in1=kr.t[:, hp * 128:(hp + 1) * 128],
                                                                       op0=ALU.mult, op1=ALU.add), reads=[srs, kr, decr], writes=[srs])
                            P.op(dve, lambda e: e.scalar_tensor_tensor(out=sgs.t[:, dr, hp * 256:(hp + 1) * 256], in0=sgs.t[:, dr, hp * 256:(hp + 1) * 256],
                                                                       scalar=decg.t[:, n, dr, hp:hp + 1], in1=kg.t[:, hp * 256:(hp + 1) * 256],
                                                                       op0=ALU.mult, op1=ALU.add), reads=[sgs, kg, decg], writes=[sgs])
                P.barrier()
            phase_B2(l, s)

        def phase_C(l, s):
            with ExitStack() as es:
                wpool = RR([P.sb(es, [128, 8, 512], BF16, "wC") for _ in range(3)])
                xt = P.sb(es, [128, 8, 512], F32, "xC")
                h = P.sb(es, [128, 8, 512], BF16, "hC")
                sq = P.sb(es, [128, 8, 512], BF16, "sqC")
                rstd = P.sb(es, [128, 512], F32, "rstdC")
                g2 = P.sb(es, [128, 8], F32, "g2")
                bm = P.sb(es, [128, 24], F32, "bm")
                P.dma(sp, g2.t[:], pd["g2"][l], writes=[g2])
                P.dma(sp, bm.t[:], pd["bm"][l], writes=[bm])
                yT = P.sb(es, [128, 3, 512], BF16, "yTc")
                qT = P.sb(es, [128, 2, 512], BF16, "qTc")
                kT = P.sb(es, [128, 2, 512], BF16, "kTc")
                fmt = {k: P.sb(es, [128, 2, 512], BF16, "c" + k) for k in ["gqf", "gkf", "gqb", "gkb"]}
                vt = P.sb(es, [128, 4, 256], BF16, "vtc")
                rgs = P.sb(es, [128, 4, 256], BF16, "rgsc")
                gvt = P.sb(es, [128, 4, 512], BF16, "gvtc")
                ggs = P.sb(es, [128, 4, 512], BF16, "ggsc")
                srt = RR([P.sb(es, [128, 2, 256], BF16, "srt") for _ in range(2)])
                sgt = RR([P.sb(es, [128, 2, 512], BF16, "sgt") for _ in range(2)])
                qfr = P.sb(es, [128, 2, 128], BF16, "qfr")
                qbr = P.sb(es, [128, 2, 128], BF16, "qbr")
                sd = RR([P.sb(es, [128, 512], BF16, "sd") for _ in range(3)])
                tmC = RR([P.sb(es, [128, 512], F32, "tmC") for _ in range(4)])
                st4 = RR([P.sb(es, [128, 16], F32, "st4") for _ in range(4)])
                rob = P.sb(es, [128, 256], BF16, "rob")
                gob = P.sb(es, [128, 512], BF16, "gob")
                roT = P.sb(es, [128, 2, 512], BF16, "roT")
                goT = P.sb(es, [128, 4, 512], BF16, "goT")
                brs = P.sb(es, [128, 8, 512], F32, "brs")
                mrg = P.sb(es, [128, 8, 512], BF16, "mrg")
                mid = P.sb(es, [128, 32, 512], BF16, "mid")
                gl = RR([P.sb(es, [128, 512], F32, "gl") for _ in range(3)])

                def wload(key, c0, n, kch=8, r0=0):
                    w = wpool.get()
                    P.dma(sp, w.t[:, 0:kch, 0:n], wview(key, l)[:, r0:r0 + kch, c0:c0 + n], writes=[w])
                    return w

                HN = int(os.environ.get("HN", "99"))

                def headnorm(ps, nh, hd, center, gate_ap, gate_T, outb):
                    W = nh * hd
                    sqt = tmC.get()
                    st = st4.get()
                    nrm = tmC.get()
                    steps = []
                    pcp = tmC.get()
                    P.op(act, lambda e: e.activation(out=pcp.t[:, 0:W], in_=ps.t[:, 0:W], func=AF.Copy), reads=[ps], writes=[pcp])
                    ps = pcp
                    steps.append(lambda: P.op(act, lambda e: e.activation(out=sqt.t[:, 0:W], in_=ps.t[:, 0:W], func=AF.Square), reads=[ps], writes=[sqt]))
                    steps.append(lambda: P.op(dve, lambda e: e.tensor_reduce(out=st.t[:, 0:nh], in_=ps.t[:, 0:W].rearrange("p (h d) -> p h d", h=nh), axis=AX.X, op=ALU.add),
                         reads=[ps], writes=[st]))
                    steps.append(lambda: P.op(dve, lambda e: e.tensor_reduce(out=st.t[:, 4:4 + nh], in_=sqt.t[:, 0:W].rearrange("p (h d) -> p h d", h=nh), axis=AX.X, op=ALU.add),
                         reads=[sqt], writes=[st]))
                    steps.append(lambda: P.op(dve, lambda e: e.tensor_scalar(out=st.t[:, 8:8 + nh], in0=st.t[:, 0:nh], scalar1=(1.0 / hd) if center else 0.0, scalar2=None, op0=ALU.mult),
                         reads=[st], writes=[st]))
                    steps.append(lambda: P.op(dve, lambda e: e.tensor_tensor(out=st.t[:, 12:12 + nh], in0=st.t[:, 8:8 + nh], in1=st.t[:, 8:8 + nh], op=ALU.mult), reads=[st], writes=[st]))
                    steps.append(lambda: P.op(dve, lambda e: e.scalar_tensor_tensor(out=st.t[:, 4:4 + nh], in0=st.t[:, 4:4 + nh], scalar=1.0 / hd, in1=st.t[:, 12:12 + nh],
                                                               op0=ALU.mult, op1=ALU.subtract), reads=[st], writes=[st]))
                    steps.append(lambda: P.op(act, lambda e: e.activation(out=st.t[:, 4:4 + nh], in_=st.t[:, 4:4 + nh], func=AF.Sqrt, bias=eps_t.t[:, 0:1]), reads=[st, eps_t], writes=[st]))
                    steps.append(lambda: P.op(dve, lambda e: e.reciprocal(out=st.t[:, 4:4 + nh], in_=st.t[:, 4:4 + nh]), reads=[st], writes=[st]))

                    def nrm_step():
                        for hh in range(nh):
                            P.op(dve, lambda e: e.tensor_scalar(out=nrm.t[:, hh * hd:(hh + 1) * hd], in0=ps.t[:, hh * hd:(hh + 1) * hd],
                                                                scalar1=st.t[:, 8 + hh:9 + hh], scalar2=st.t[:, 4 + hh:5 + hh], op0=ALU.subtract, op1=ALU.mult),
                                 reads=[ps, st], writes=[nrm])
                    steps.append(nrm_step)
                    steps.append(lambda: P.op(pool, lambda e: e.tensor_tensor(out=outb.t[:, 0:W], in0=nrm.t[:, 0:W], in1=gate_ap, op=ALU.mult), reads=[nrm, gate_T], writes=[outb]))
                    for i, f in enumerate(steps):
                        if i < HN:
                            f()

                for ti in range(NT):
                    t0 = ti * 512
                    P.dma(sp, xt.t[:], XT[s, :, :, t0:t0 + 512].rearrange("k p t -> p k t"), writes=[xt])
                    P.dma(sp, h.t[:], HS[:, :, t0:t0 + 512].rearrange("k p t -> p k t"), writes=[h])
                    P.dma(sp, yT.t[:], YTS[:, :, t0:t0 + 512].rearrange("k p t -> p k t"), writes=[yT])
                    P.dma(sp, qT.t[:], FMS["rq"][:, :, t0:t0 + 512].rearrange("k p t -> p k t"), writes=[qT])
                    P.dma(sp, kT.t[:], FMS["rkt"][:, :, t0:t0 + 512].rearrange("k p t -> p k t"), writes=[kT])
                    for k in fmt:
                        P.dma(sp, fmt[k].t[:], FMS[k][:, :, t0:t0 + 512].rearrange("k p t -> p k t"), writes=[fmt[k]])
                    for tl, src in ((vt, RV), (rgs, RGS), (gvt, GV), (ggs, GGS)):
                        P.dma(sp, tl.t[:], src[t0:t0 + 512, :].rearrange("(c p) f -> p c f", p=128), writes=[tl])
                    for c in range(4):
                        ch = ti * 4 + c
                        cs = slice(c * 128, (c + 1) * 128)
                        sr = srt.get()
                        sg = sgt.get()
                        P.dma(sp, sr.t[:], SR[ch], writes=[sr])
                        P.dma(sp, sg.t[:], SG[ch], writes=[sg])
                        if CST < 2:
                            continue
                        ps = psf.get()
                        for hd_ in range(4):
                            hp, hh = hd_ // 2, hd_ % 2
                            rs = slice(hh * 64, (hh + 1) * 64)
                            P.mm(ps.t[:, hd_ * 128:(hd_ + 1) * 128], [(kT.t[rs, hp, cs], qT.t[rs, hp, cs])], reads=[kT, qT], writes=[ps])
                        sdt = sd.get()
                        P.op(dve, lambda e: e.tensor_tensor(out=sdt.t[:], in0=ps.t[:], in1=dmat.t[:], op=ALU.mult), reads=[ps, dmat], writes=[sdt])
                        P.op(pool, lambda e: e.tensor_tensor(out=qfr.t[:], in0=qT.t[:, :, cs], in1=qft.t[:], op=ALU.mult), reads=[qT, qft], writes=[qfr])
                        P.op(pool, lambda e: e.tensor_tensor(out=qbr.t[:], in0=qT.t[:, :, cs], in1=qbt.t[:], op=ALU.mult), reads=[qT, qbt], writes=[qbr])
                        po = psf.get()
                        for hp in range(2):
                            P.mm(po.t[:, hp * 128:(hp + 1) * 128],
                                 [(qfr.t[:, hp, :], sr.t[:, 0, hp * 128:(hp + 1) * 128]), (qbr.t[:, hp, :], sr.t[:, 1, hp * 128:(hp + 1) * 128])],
                                 reads=[qfr, qbr, sr], writes=[po], stop=False)
                            for hh in range(2):
                                hd_ = 2 * hp + hh
                                P.mm(po.t[:, hd_ * 64:(hd_ + 1) * 64], [(sdt.t[:, hd_ * 128:(hd_ + 1) * 128], vt.t[:, c, hd_ * 64:(hd_ + 1) * 64])],
                                     reads=[sdt, vt], writes=[po], start=False, stop=(hh == 1))
                        if CST == 2 and os.environ.get('NOHN'):
                            continue
                        headnorm(po, 4, 64, True, rgs.t[:, c, :], rgs, rob)
                        if CST == 2 and os.environ.get('NOTR'):
                            continue
                        pt = psb.get()
                        for k in range(2):
                            P.op(pe, lambda e: e.transpose(out=pt.t[:, k * 128:(k + 1) * 128], in_=rob.t[:, k * 128:(k + 1) * 128], identity=ident_b.t[:]),
                                 reads=[rob, ident_b], writes=[pt])
                        P.op(act, lambda e: e.activation(out=roT.t[:, :, cs], in_=pt.t[:, 0:256].rearrange("p (k t) -> p k t", k=2), func=AF.Copy), reads=[pt], writes=[roT])
                        if CST < 3:
                            continue
                        sdf = sd.get()
                        sdb = sd.get()
                        for dr, (kk, qq, mk, dst) in enumerate(((fmt["gkf"], fmt["gqf"], m_le4, sdf), (fmt["gkb"], fmt["gqb"], m_gt4, sdb))):
                            ps = psf.get()
                            for hd_ in range(4):
                                hp, hh = hd_ // 2, hd_ % 2
                                rs = slice(hh * 64, (hh + 1) * 64)
                                P.mm(ps.t[:, hd_ * 128:(hd_ + 1) * 128], [(kk.t[rs, hp, cs], qq.t[rs, hp, cs])], reads=[kk, qq], writes=[ps])
                            P.op(dve, lambda e: e.tensor_tensor(out=dst.t[:], in0=ps.t[:], in1=mk.t[:], op=ALU.mult), reads=[ps, mk], writes=[dst])
                        po = psf.get()
                        for hp in range(2):
                            P.mm(po.t[:, hp * 256:(hp + 1) * 256],
                                 [(fmt["gqf"].t[:, hp, cs], sg.t[:, 0, hp * 256:(hp + 1) * 256]), (fmt["gqb"].t[:, hp, cs], sg.t[:, 1, hp * 256:(hp + 1) * 256])],
                                 reads=[fmt["gqf"], fmt["gqb"], sg], writes=[po], stop=False)
                            for hh in range(2):
                                hd_ = 2 * hp + hh
                                P.mm(po.t[:, hd_ * 128:(hd_ + 1) * 128],
                                     [(sdf.t[:, hd_ * 128:(hd_ + 1) * 128], gvt.t[:, c, hd_ * 128:(hd_ + 1) * 128]),
                                      (sdb.t[:, hd_ * 128:(hd_ + 1) * 128], gvt.t[:, c, hd_ * 128:(hd_ + 1) * 128])],
                                     reads=[sdf, sdb, gvt], writes=[po], start=False, stop=(hh == 1))
                        headnorm(po, 4, 128, False, ggs.t[:, c, :], ggs, gob)
                        pt = psb.get()
                        for k in range(4):
                            P.op(pe, lambda e: e.transpose(out=pt.t[:, k * 128:(k + 1) * 128], in_=gob.t[:, k * 128:(k + 1) * 128], identity=ident_b.t[:]),
                                 reads=[gob, ident_b], writes=[pt])
                        P.op(act, lambda e: e.activation(out=goT.t[:, :, cs], in_=pt.t[:, 0:512].rearrange("p (k t) -> p k t", k=4), func=AF.Copy), reads=[pt], writes=[goT])
                    if CST < 4:
                        continue
                    def gate_ps(wm_t, f):
                        ps = psf.get()
                        P.mm(ps.t[:, :], [(wm_t.t[:, k, f * 128:(f + 1) * 128], h.t[:, k, :]) for k in range(8)], reads=[wm_t, h], writes=[ps])
                        return ps
                    for fb in range(2):
                        wa = wload("wa", fb * 512, 512, kch=2)
                        wc = wload("wc", fb * 512, 512, kch=4)
                        wga = wload("wm", fb * 512, 512)
                        for f in range(4):
                            fo = fb * 4 + f
                            g = gl.get()
                            ps = gate_ps(wga, f)
                            P.op(act, lambda e: e.activation(out=g.t[:], in_=ps.t[:], func=AF.Sigmoid, bias=bm.t[:, fo:fo + 1]), reads=[ps, bm], writes=[g])
                            pa = psf.get()
                            P.mm(pa.t[:, :], [(wa.t[:, k, f * 128:(f + 1) * 128], roT.t[:, k, :]) for k in range(2)], reads=[wa, roT], writes=[pa])
                            P.op(dve, lambda e: e.tensor_tensor(out=brs.t[:, fo, :], in0=pa.t[:], in1=g.t[:], op=ALU.mult), reads=[pa, g], writes=[brs])
                        wgc = wload("wm", 2048 + fb * 512, 512)
                        for f in range(4):
                            fo = fb * 4 + f
                            g = gl.get()
                            ps = gate_ps(wgc, f)
                            P.op(act, lambda e: e.activation(out=g.t[:], in_=ps.t[:], func=AF.Sigmoid, bias=bm.t[:, 16 + fo:17 + fo]), reads=[ps, bm], writes=[g])
                            pc = psf.get()
                            P.mm(pc.t[:, :], [(wc.t[:, k, f * 128:(f + 1) * 128], goT.t[:, k, :]) for k in range(4)], reads=[wc, goT], writes=[pc])
                            tmp = tmC.get()
                            P.op(dve, lambda e: e.tensor_tensor(out=tmp.t[:], in0=pc.t[:], in1=g.t[:], op=ALU.mult), reads=[pc, g], writes=[tmp])
                            P.op(pool, lambda e: e.tensor_tensor(out=brs.t[:, fo, :], in0=brs.t[:, fo, :], in1=tmp.t[:], op=ALU.add), reads=[brs, tmp], writes=[brs])
                        wgb = wload("wm", 1024 + fb * 512, 512)
                        wb1 = wload("wb2", fb * 512, 512, kch=3)
                        wb2_ = wload("wb2", 1024 + fb * 512, 512, kch=3)
                        for f in range(4):
                            fo = fb * 4 + f
                            g = gl.get()
                            ps = gate_ps(wgb, f)
                            P.op(act, lambda e: e.activation(out=g.t[:], in_=ps.t[:], func=AF.Sigmoid, bias=bm.t[:, 8 + fo:9 + fo]), reads=[ps, bm], writes=[g])
                            p1 = psf.get()
                            P.mm(p1.t[:, :], [(wb1.t[:, k, f * 128:(f + 1) * 128], yT.t[:, k, :]) for k in range(3)], reads=[wb1, yT], writes=[p1])
                            p2 = psf.get()
                            P.mm(p2.t[:, :], [(wb2_.t[:, k, f * 128:(f + 1) * 128], yT.t[:, k, :]) for k in range(3)], reads=[wb2_, yT], writes=[p2])
                            sgm = tmC.get()
                            P.op(act, lambda e: e.activation(out=sgm.t[:], in_=p2.t[:], func=AF.Sigmoid), reads=[p2], writes=[sgm])
                            P.op(pool, lambda e: e.tensor_tensor(out=sgm.t[:], in0=sgm.t[:], in1=g.t[:], op=ALU.mult), reads=[sgm, g], writes=[sgm])
                            tmp = tmC.get()
                            P.op(dve, lambda e: e.tensor_tensor(out=tmp.t[:], in0=p1.t[:], in1=sgm.t[:], op=ALU.mult), reads=[p1, sgm], writes=[tmp])
                            P.op(pool, lambda e: e.tensor_tensor(out=mrg.t[:, fo, :], in0=brs.t[:, fo, :], in1=tmp.t[:], op=ALU.add), reads=[brs, tmp], writes=[mrg])
                    if CST < 5:
                        continue
                    for fb in range(2):
                        wo = wload("wo", fb * 512, 512)
                        for f in range(4):
                            fo = fb * 4 + f
                            ps = psf.get()
                            P.mm(ps.t[:, :], [(wo.t[:, k, f * 128:(f + 1) * 128], mrg.t[:, k, :]) for k in range(8)], reads=[wo, mrg], writes=[ps])
                            P.op(dve, lambda e: e.tensor_tensor(out=xt.t[:, fo, :], in0=xt.t[:, fo, :], in1=ps.t[:], op=ALU.add), reads=[xt, ps], writes=[xt])
                    if CST < 6:
                        continue
                    rmsnorm(es, xt, g2, h, sq, rstd)
                    for fb in range(8):
                        w1 = wload("wf1", fb * 512, 512)
                        for f in range(4):
                            fo = fb * 4 + f
                            ps = psf.get()
                            P.mm(ps.t[:, :], [(w1.t[:, k, f * 128:(f + 1) * 128], h.t[:, k, :]) for k in range(8)], reads=[w1, h], writes=[ps])
                            r = tmC.get()
                            P.op(act, lambda e: e.activation(out=r.t[:], in_=ps.t[:], func=AF.Relu), reads=[ps], writes=[r])
                            P.op(pool, lambda e: e.tensor_tensor(out=mid.t[:, fo, :], in0=r.t[:], in1=r.t[:], op=ALU.mult), reads=[r], writes=[mid])
                    for fb in range(2):
                        pss = [psf.get() for _ in range(4)]
                        for kb in range(4):
                            w2 = wload("wf2", fb * 512, 512, kch=8, r0=kb * 8)
                            for f in range(4):
                                P.mm(pss[f].t[:, :], [(w2.t[:, k, f * 128:(f + 1) * 128], mid.t[:, kb * 8 + k, :]) for k in range(8)],
                                     reads=[w2, mid], writes=[pss[f]], start=(kb == 0), stop=(kb == 3))
                        for f in range(4):
                            fo = fb * 4 + f
                            P.op(dve, lambda e: e.tensor_tensor(out=xt.t[:, fo, :], in0=xt.t[:, fo, :], in1=pss[f].t[:], op=ALU.add), reads=[xt, pss[f]], writes=[xt])
                    P.dma(sp, XT[s, :, :, t0:t0 + 512].rearrange("k p t -> p k t"), xt.t[:], reads=[xt])
                P.barrier()

        for l in range(DEPTH):
            for s in range(NSEQ):
                import os
                ph = os.environ.get("PH", "ABC")
                if "A" in ph:
                    phase_A(l, s)
                if "B" in ph:
                    phase_B(l, s)
                if "C" in ph:
                    phase_C(l, s)

        with ExitStack() as es:
            xp = RR([P.sb(es, [128, 8, 512], F32, "fx") for _ in range(2)])
            yo = RR([P.sb(es, [128, 8, 512], F32, "fy") for _ in range(2)])
            sq = P.sb(es, [128, 8, 512], BF16, "fsq")
            rstd = P.sb(es, [128, 512], F32, "frstd")
            ot = RR([P.sb(es, [128, D], F32, "fo") for _ in range(2)])
            for s in range(NSEQ):
                for ti in range(NT):
                    xt = xp.get()
                    P.dma(sp, xt.t[:], XT[s, :, :, ti * 512:(ti + 1) * 512].rearrange("k p t -> p k t"), writes=[xt])
                    y = yo.get()
                    rmsnorm(es, xt, gfin, y, sq, rstd)
                    for c in range(4):
                        o_t = ot.get()
                        for half in range(2):
                            ps = psf.get()
                            for kk in range(4):
                                k = half * 4 + kk
                                P.op(pe, lambda e, k=k, kk=kk: e.transpose(out=ps.t[:, kk * 128:(kk + 1) * 128],
                                                                             in_=y.t[:, k, c * 128:(c + 1) * 128], identity=ident_f.t[:]),
                                     reads=[y, ident_f], writes=[ps])
                            if half == 0:
                                P.op(act, lambda e: e.activation(out=o_t.t[:, 0:512], in_=ps.t[:], func=AF.Copy), reads=[ps], writes=[o_t])
                            else:
                                P.op(dve, lambda e: e.tensor_copy(out=o_t.t[:, 512:1024], in_=ps.t[:]), reads=[ps], writes=[o_t])
                        r0 = ti * 512 + c * 128
                        P.dma(sp, out_d[s, r0:r0 + 128, :], o_t.t[:], reads=[o_t])
            P.barrier()
        print('INSTR', {e.name: e.count for e in P.engs}, 'dma', sum(P.dma_val) // 16, flush=True)
    return nc


def kernel(**inputs):
    inp = {k: np.asarray(v) for k, v in inputs.items()}
    B, L, _ = inp["x"].shape
    nseq = B // NCORES
    consts = host_consts(L)
    lp = host_layer_params(inp, DEPTH_FULL)
    nc = build(L, DEPTH_FULL, nseq, consts, lp)
    in_maps = []
    for c in range(NCORES):
        m = {"x": np.ascontiguousarray(inp["x"][c * nseq:(c + 1) * nseq])}
        m.update({"c_" + k: v for k, v in consts.items()})
        m.update({"p_" + k: v for k, v in lp.items()})
        in_maps.append(m)
    res = run_bass_kernel_spmd(nc, in_maps, core_ids=list(range(NCORES)))
    return np.concatenate([r["out"] for r in res.results], axis=0).astype(np.float32)
```
